# Optimizing a Trainium2 kernel written in Bass

```python
import jax, jax.numpy as jnp
from jax import lax
import numpy as np

D_MODEL = 1024
BATCH = 8
SEQ = 2048
DEPTH = 2

EPS = 1e-6
NEG_INF = -1e30
FORCE_SCORE = 1e4
N_BRANCH = 3

GMLP_WIDTH = 1024
GMLP_GROUPS = 4
GMLP_GROUP_DIM = GMLP_WIDTH // GMLP_GROUPS
GMLP_CHUNK = 128

NSA_HEADS = 16
NSA_KV_GROUPS = 4
NSA_HEAD_DIM = 64
NSA_HPG = NSA_HEADS // NSA_KV_GROUPS
NSA_Q_WIDTH = NSA_HEADS * NSA_HEAD_DIM
NSA_KV_WIDTH = NSA_KV_GROUPS * NSA_HEAD_DIM
CMP_BLOCK = 32
CMP_STRIDE = 16
CMP_HIDDEN = 256
SEL_BLOCK = 64
SEL_TOP_N = 16
WINDOW = 512
NSA_Q_CHUNK = 64

RNN_WIDTH = 1024
RNN_HEADS = 16
RNN_HEAD_DIM = RNN_WIDTH // RNN_HEADS
CONV_WIDTH = 4
LRU_C = 8.0

D_FF = -(-8 * D_MODEL // (3 * 256)) * 256

IN_SPLITS = (GMLP_WIDTH, GMLP_WIDTH, NSA_Q_WIDTH, 6 * NSA_KV_WIDTH, N_BRANCH * NSA_HEADS, RNN_WIDTH, RNN_WIDTH, N_BRANCH * D_MODEL)
D_IN = sum(IN_SPLITS)

kernel_name = "hybrid_gmlp_nsa_rglru_block"


def rms_norm(x, g):
    xf = x.astype(jnp.float32)
    y = xf * lax.rsqrt(jnp.mean(xf * xf, axis=-1, keepdims=True) + EPS)
    return (y * g.astype(jnp.float32)).astype(x.dtype)


def layer_norm(x, g, b):
    xf = x.astype(jnp.float32)
    mu = jnp.mean(xf, axis=-1, keepdims=True)
    var = jnp.mean(jnp.square(xf - mu), axis=-1, keepdims=True)
    y = (xf - mu) * lax.rsqrt(var + EPS) * g.astype(jnp.float32) + b.astype(jnp.float32)
    return y.astype(x.dtype)


def gmlp_mixer(u, v, ln_g, ln_b, w_s, b_s):
    B, S, _ = u.shape
    n_chunks = S // GMLP_CHUNK
    vn = layer_norm(v, ln_g, ln_b).reshape(B, n_chunks, GMLP_CHUNK, GMLP_GROUPS, GMLP_GROUP_DIM)
    causal = jnp.tril(jnp.ones((GMLP_CHUNK, GMLP_CHUNK), w_s.dtype))
    w = w_s * causal
    mixed = jnp.einsum('gts,bnsgd->bntgd', w, vn) + b_s.T[None, None, :, :, None]
    return u * mixed.reshape(B, S, GMLP_WIDTH)


def nsa_mixer(q, k_cmp, v_cmp, k_slc, v_slc, k_win, v_win, branch_gates,
              pe_k, pe_v, wk1, wk2, wv1, wv2):
    B, S, _ = q.shape
    G, HP, DH = NSA_KV_GROUPS, NSA_HPG, NSA_HEAD_DIM
    scale = DH ** -0.5
    qh = q.reshape(B, S, G, HP, DH).transpose(0, 2, 3, 1, 4)

    def kv_heads(t):
        return t.reshape(B, S, G, DH).transpose(0, 2, 1, 3)

    k_cmp, v_cmp, k_slc, v_slc, k_win, v_win = map(kv_heads, (k_cmp, v_cmp, k_slc, v_slc, k_win, v_win))
    pos = jnp.arange(S)

    n_cmp = (S - CMP_BLOCK) // CMP_STRIDE + 1
    blk_idx = jnp.arange(n_cmp)[:, None] * CMP_STRIDE + jnp.arange(CMP_BLOCK)[None, :]

    def compress(t, pe, w1, w2):
        blocks = t[:, :, blk_idx] + pe
        flat = blocks.reshape(B, G, n_cmp, CMP_BLOCK * DH)
        return jax.nn.gelu(flat @ w1) @ w2

    kc = compress(k_cmp, pe_k, wk1, wk2)
    vc = compress(v_cmp, pe_v, wv1, wv2)
    cmp_end = jnp.arange(n_cmp) * CMP_STRIDE + CMP_BLOCK - 1
    valid_c = cmp_end[None, :] <= pos[:, None]
    s_c = jnp.einsum('bghsd,bgnd->bghsn', qh, kc).astype(jnp.float32) * scale
    s_c = jnp.where(valid_c, s_c, NEG_INF)
    p_c = jax.nn.softmax(s_c, axis=-1) * jnp.any(valid_c, axis=-1)[:, None].astype(jnp.float32)
    o_cmp = jnp.einsum('bghsn,bgnd->bghsd', p_c.astype(vc.dtype), vc)

    n_sel = S // SEL_BLOCK
    c_start = np.arange(n_cmp) * CMP_STRIDE
    s_start = np.arange(n_sel) * SEL_BLOCK
    overlap = np.clip(np.minimum(c_start[:, None] + CMP_BLOCK, s_start[None, :] + SEL_BLOCK)
                      - np.maximum(c_start[:, None], s_start[None, :]), 0, None) / CMP_BLOCK
    overlap = jnp.asarray(overlap, jnp.float32)
    imp = jnp.einsum('bghsn,nj->bgsj', p_c, overlap)
    cur = (pos // SEL_BLOCK)[:, None]
    j = jnp.arange(n_sel)[None, :]
    forced = (j == 0) | (j == cur) | (j == cur - 1)
    imp = jnp.where(forced, FORCE_SCORE, jnp.where(j > cur, NEG_INF, imp))
    top_n = min(SEL_TOP_N, n_sel)
    _, sel_idx = lax.top_k(imp, top_n)

    k_sb = k_slc.reshape(B, G, n_sel, SEL_BLOCK, DH)
    v_sb = v_slc.reshape(B, G, n_sel, SEL_BLOCK, DH)
    pad = jnp.zeros((B, G, WINDOW, DH), k_win.dtype)
    k_wp = jnp.concatenate([pad, k_win], axis=2)
    v_wp = jnp.concatenate([pad, v_win], axis=2)
    gather = jax.vmap(jax.vmap(lambda blocks, ix: blocks[ix]))

    C = NSA_Q_CHUNK
    n_chunks = S // C
    q_ch = jnp.moveaxis(qh.reshape(B, G, HP, n_chunks, C, DH), 3, 0)
    idx_ch = jnp.moveaxis(sel_idx.reshape(B, G, n_chunks, C, top_n), 2, 0)

    def chunk_fn(args):
        c, qc, ic = args
        t = c * C + jnp.arange(C)
        kg = gather(k_sb, ic)
        vg = gather(v_sb, ic)
        kpos = ic[..., None] * SEL_BLOCK + jnp.arange(SEL_BLOCK)
        s = jnp.einsum('bghcd,bgckld->bghckl', qc, kg).astype(jnp.float32) * scale
        s = jnp.where((kpos <= t[:, None, None])[:, :, None], s, NEG_INF)
        p = jax.nn.softmax(s.reshape(B, G, HP, C, top_n * SEL_BLOCK), axis=-1).reshape(s.shape)
        o_s = jnp.einsum('bghckl,bgckld->bghcd', p.astype(vg.dtype), vg)
        kw = lax.dynamic_slice_in_dim(k_wp, c * C, WINDOW + C, axis=2)
        vw = lax.dynamic_slice_in_dim(v_wp, c * C, WINDOW + C, axis=2)
        wpos = c * C - WINDOW + jnp.arange(WINDOW + C)
        wmask = (wpos[None, :] >= 0) & (wpos[None, :] <= t[:, None]) & (wpos[None, :] > t[:, None] - WINDOW)
        sw = jnp.einsum('bghcd,bgkd->bghck', qc, kw).astype(jnp.float32) * scale
        sw = jnp.where(wmask, sw, NEG_INF)
        o_w = jnp.einsum('bghck,bgkd->bghcd', jax.nn.softmax(sw, axis=-1).astype(vw.dtype), vw)
        return o_s, o_w

    o_slc, o_win = lax.map(chunk_fn, (jnp.arange(n_chunks), q_ch, idx_ch))
    o_slc = jnp.moveaxis(o_slc, 0, 3).reshape(B, G, HP, S, DH)
    o_win = jnp.moveaxis(o_win, 0, 3).reshape(B, G, HP, S, DH)

    g = jax.nn.sigmoid(branch_gates).reshape(B, S, G, HP, N_BRANCH).transpose(0, 2, 3, 1, 4)
    o = g[..., 0:1] * o_cmp + g[..., 1:2] * o_slc + g[..., 2:3] * o_win
    return o.transpose(0, 3, 1, 2, 4).reshape(B, S, NSA_Q_WIDTH)


def rglru_mixer(xr, gate, conv_w, conv_b, w_a, b_a, w_x, b_x, lam):
    B, S, _ = xr.shape
    xc = lax.conv_general_dilated(xr, conv_w[:, None, :], window_strides=(1,),
                                  padding=[(CONV_WIDTH - 1, 0)],
                                  dimension_numbers=('NWC', 'WIO', 'NWC'),
                                  feature_group_count=RNN_WIDTH) + conv_b
    xh = xc.reshape(B, S, RNN_HEADS, RNN_HEAD_DIM)
    r = jax.nn.sigmoid(jnp.einsum('bshi,hio->bsho', xh, w_a).reshape(B, S, RNN_WIDTH) + b_a)
    i = jax.nn.sigmoid(jnp.einsum('bshi,hio->bsho', xh, w_x).reshape(B, S, RNN_WIDTH) + b_x)
    log_a = -LRU_C * r.astype(jnp.float32) * jax.nn.softplus(-lam.astype(jnp.float32))
    a = jnp.exp(log_a)
    b_in = jnp.sqrt(-jnp.expm1(2.0 * log_a)) * (i * xc).astype(jnp.float32)

    def combine(left, right):
        a1, b1 = left
        a2, b2 = right
        return a2 * a1, a2 * b1 + b2

    _, h = lax.associative_scan(combine, (a, b_in), axis=1)
    return h.astype(xr.dtype) * jax.nn.gelu(gate)


def hybrid_layer(x, g_pre_mix, g_post_mix, g_pre_ffn, g_post_ffn, w_in,
                 gmlp_ln_g, gmlp_ln_b, gmlp_ws, gmlp_bs,
                 nsa_pe_k, nsa_pe_v, nsa_wk1, nsa_wk2, nsa_wv1, nsa_wv2,
                 rnn_conv_w, rnn_conv_b, rnn_wa, rnn_ba, rnn_wx, rnn_bx, rnn_lam,
                 w_br_a, w_br_b, w_br_c, w_o, w_ffn_in, w_ffn_out):
    B, S, _ = x.shape
    h = rms_norm(x, g_pre_mix)
    z = h @ w_in
    split_points = [int(p) for p in np.cumsum(IN_SPLITS)[:-1]]
    u, v, q, kv, nsa_g, xr, rg, mg = jnp.split(z, split_points, axis=-1)
    k_cmp, v_cmp, k_slc, v_slc, k_win, v_win = jnp.split(kv, 6, axis=-1)

    y_a = gmlp_mixer(jax.nn.gelu(u), jax.nn.gelu(v), gmlp_ln_g, gmlp_ln_b, gmlp_ws, gmlp_bs)
    y_b = nsa_mixer(q, k_cmp, v_cmp, k_slc, v_slc, k_win, v_win, nsa_g,
                    nsa_pe_k, nsa_pe_v, nsa_wk1, nsa_wk2, nsa_wv1, nsa_wv2)
    y_c = rglru_mixer(xr, rg, rnn_conv_w, rnn_conv_b, rnn_wa, rnn_ba, rnn_wx, rnn_bx, rnn_lam)

    gates = jax.nn.sigmoid(mg).reshape(B, S, N_BRANCH, D_MODEL)
    merged = (gates[:, :, 0] * (y_a @ w_br_a)
              + gates[:, :, 1] * (y_b @ w_br_b)
              + gates[:, :, 2] * (y_c @ w_br_c))
    x = x + rms_norm(merged @ w_o, g_post_mix)

    hf = rms_norm(x, g_pre_ffn)
    f_gate, f_up = jnp.split(hf @ w_ffn_in, 2, axis=-1)
    f = (jax.nn.silu(f_gate) * f_up) @ w_ffn_out
    return x + rms_norm(f, g_post_ffn)


def setup_inputs(seed: int = 0) -> dict:
    key = jax.random.key(seed)
    keys = iter(jax.random.split(key, 48))
    L = DEPTH

    def nrm(shape, scale):
        return jax.random.normal(next(keys), shape, jnp.float32) * scale

    def gain(n):
        return 1.0 + nrm((L, n), 0.05)

    u_lam = jax.random.uniform(next(keys), (L, RNN_WIDTH), jnp.float32, minval=0.9, maxval=0.999)
    a0 = u_lam ** (1.0 / LRU_C)
    rnn_lam = jnp.log(a0) - jnp.log1p(-a0)

    return {
        "x": nrm((BATCH, SEQ, D_MODEL), 1.0),
        "g_pre_mix": gain(D_MODEL),
        "g_post_mix": gain(D_MODEL),
        "g_pre_ffn": gain(D_MODEL),
        "g_post_ffn": gain(D_MODEL),
        "w_in": nrm((L, D_MODEL, D_IN), D_MODEL ** -0.5),
        "gmlp_ln_g": gain(GMLP_WIDTH),
        "gmlp_ln_b": nrm((L, GMLP_WIDTH), 0.05),
        "gmlp_ws": nrm((L, GMLP_GROUPS, GMLP_CHUNK, GMLP_CHUNK), GMLP_CHUNK ** -0.5),
        "gmlp_bs": 1.0 + nrm((L, GMLP_GROUPS, GMLP_CHUNK), 0.1),
        "nsa_pe_k": nrm((L, CMP_BLOCK, NSA_HEAD_DIM), 0.1),
        "nsa_pe_v": nrm((L, CMP_BLOCK, NSA_HEAD_DIM), 0.1),
        "nsa_wk1": nrm((L, CMP_BLOCK * NSA_HEAD_DIM, CMP_HIDDEN), (CMP_BLOCK * NSA_HEAD_DIM) ** -0.5),
        "nsa_wk2": nrm((L, CMP_HIDDEN, NSA_HEAD_DIM), CMP_HIDDEN ** -0.5),
        "nsa_wv1": nrm((L, CMP_BLOCK * NSA_HEAD_DIM, CMP_HIDDEN), (CMP_BLOCK * NSA_HEAD_DIM) ** -0.5),
        "nsa_wv2": nrm((L, CMP_HIDDEN, NSA_HEAD_DIM), CMP_HIDDEN ** -0.5),
        "rnn_conv_w": nrm((L, CONV_WIDTH, RNN_WIDTH), CONV_WIDTH ** -0.5),
        "rnn_conv_b": nrm((L, RNN_WIDTH), 0.05),
        "rnn_wa": nrm((L, RNN_HEADS, RNN_HEAD_DIM, RNN_HEAD_DIM), RNN_HEAD_DIM ** -0.5),
        "rnn_ba": nrm((L, RNN_WIDTH), 0.1),
        "rnn_wx": nrm((L, RNN_HEADS, RNN_HEAD_DIM, RNN_HEAD_DIM), RNN_HEAD_DIM ** -0.5),
        "rnn_bx": nrm((L, RNN_WIDTH), 0.1),
        "rnn_lam": rnn_lam,
        "w_br_a": nrm((L, GMLP_WIDTH, D_MODEL), GMLP_WIDTH ** -0.5),
        "w_br_b": nrm((L, NSA_Q_WIDTH, D_MODEL), NSA_Q_WIDTH ** -0.5),
        "w_br_c": nrm((L, RNN_WIDTH, D_MODEL), RNN_WIDTH ** -0.5),
        "w_o": nrm((L, D_MODEL, D_MODEL), D_MODEL ** -0.5),
        "w_ffn_in": nrm((L, D_MODEL, 2 * D_FF), D_MODEL ** -0.5),
        "w_ffn_out": nrm((L, D_FF, D_MODEL), D_FF ** -0.5),
    }


def reference(x, g_pre_mix, g_post_mix, g_pre_ffn, g_post_ffn, w_in,
              gmlp_ln_g, gmlp_ln_b, gmlp_ws, gmlp_bs,
              nsa_pe_k, nsa_pe_v, nsa_wk1, nsa_wk2, nsa_wv1, nsa_wv2,
              rnn_conv_w, rnn_conv_b, rnn_wa, rnn_ba, rnn_wx, rnn_bx, rnn_lam,
              w_br_a, w_br_b, w_br_c, w_o, w_ffn_in, w_ffn_out):
    for l in range(DEPTH):
        x = hybrid_layer(x, g_pre_mix[l], g_post_mix[l], g_pre_ffn[l], g_post_ffn[l], w_in[l],
                         gmlp_ln_g[l], gmlp_ln_b[l], gmlp_ws[l], gmlp_bs[l],
                         nsa_pe_k[l], nsa_pe_v[l], nsa_wk1[l], nsa_wk2[l], nsa_wv1[l], nsa_wv2[l],
                         rnn_conv_w[l], rnn_conv_b[l], rnn_wa[l], rnn_ba[l], rnn_wx[l], rnn_bx[l], rnn_lam[l],
                         w_br_a[l], w_br_b[l], w_br_c[l], w_o[l], w_ffn_in[l], w_ffn_out[l])
    return x
```

```python
import numpy as np
from contextlib import ExitStack
import concourse.bass as bass
import concourse.mybir as mybir
from concourse.bass_utils import run_bass_kernel_spmd

F32 = mybir.dt.float32
BF16 = mybir.dt.bfloat16
AF = mybir.ActivationFunctionType
ALU = mybir.AluOpType
AX = mybir.AxisListType

S = 2048
D = 1024
NT = 16
NQ = 4
KC = 8
DEPTH = 2
DFF = 2816
NJ = 22
D_IN = 9776
OFF_U, OFF_V, OFF_Q, OFF_KV, OFF_NG, OFF_XR, OFF_RG, OFF_MG = 0, 1024, 2048, 3072, 4608, 4656, 5680, 6704
NCMP = 127
EPS = 1e-6
NEGM = 30000.0
BSTOP = 99


class Buf:
    __slots__ = ("name", "w", "r", "sem", "dcnt")

    def __init__(self, name):
        self.name = name
        self.w = None
        self.r = {}
        self.sem = None
        self.dcnt = 0


class Eng:
    def __init__(self, name, be, sem):
        self.name, self.be, self.sem = name, be, sem
        self.cnt = 0
        self.seen = {}


class Sched:
    def __init__(self, nc, stack):
        self.nc = nc
        self.stack = stack
        self.engs = {}
        for name, be in (("pe", nc.tensor), ("act", nc.scalar), ("dve", nc.vector),
                         ("pool", nc.gpsimd), ("sp", nc.sync)):
            sem = stack.enter_context(nc.semaphore("sem_" + name))
            self.engs[name] = Eng(name, be, sem)
        self.dma_bufs = []
        self.nwait = 0

    def _wait(self, E, ev):
        sem, val = ev
        k = id(sem)
        if E.seen.get(k, 0) >= val:
            return
        E.be.wait_ge(sem, val)
        E.seen[k] = val
        self.nwait += 1

    def _deps(self, E, reads, writes):
        for b in reads:
            if b.w is not None:
                if b.w[0] is E.sem and E.name == "pe":
                    continue
                self._wait(E, b.w)
        for b in writes:
            if b.w is not None and not (b.w[0] is E.sem and E.name == "pe"):
                self._wait(E, b.w)
            for ev in b.r.values():
                if not (ev[0] is E.sem and E.name == "pe"):
                    self._wait(E, ev)

    def _record(self, ev, reads, writes):
        k = id(ev[0])
        for b in reads:
            old = b.r.get(k)
            if old is None or old[1] < ev[1]:
                b.r[k] = ev
        for b in writes:
            b.w = ev
            b.r = {}

    def op(self, eng, fn, reads=(), writes=(), inc=True):
        E = self.engs[eng]
        self._deps(E, reads, writes)
        ins = fn(E.be)
        if inc:
            ins.then_inc(E.sem, 1)
            E.cnt += 1
            ev = (E.sem, E.cnt)
        else:
            ev = (E.sem, E.cnt + 1)
        self._record(ev, reads, writes)
        return ins

    def dma(self, q, out_ap, in_ap, reads, writes, sb, **kw):
        E = self.engs[q]
        self._deps(E, reads, writes)
        if sb.sem is None:
            sb.sem = self.stack.enter_context(self.nc.semaphore(f"dsem_{sb.name}_{_uid()}"))
            self.dma_bufs.append(sb)
        E.be.dma_start(out=out_ap, in_=in_ap, **kw).then_inc(sb.sem, 16)
        sb.dcnt += 1
        ev = (sb.sem, 16 * sb.dcnt)
        self._record(ev, reads, writes)

    def barrier(self):
        evs = [(E.sem, E.cnt) for E in self.engs.values() if E.cnt > 0]
        evs += [(b.sem, 16 * b.dcnt) for b in self.dma_bufs if b.dcnt > 0]
        for E in self.engs.values():
            for ev in evs:
                if ev[0] is E.sem:
                    continue
                self._wait(E, ev)

    def finish(self):
        E = self.engs["sp"]
        for b in self.dma_bufs:
            if b.dcnt > 0:
                self._wait(E, (b.sem, 16 * b.dcnt))
        for E2 in self.engs.values():
            if E2 is not E and E2.cnt > 0:
                self._wait(E, (E2.sem, E2.cnt))


_UID = [0]


def _uid():
    _UID[0] += 1
    return _UID[0]


def skew(n, stages, lag=1):
    ns = len(stages)
    for t in range(n + (ns - 1) * lag):
        for k in range(ns - 1, -1, -1):
            i = t - k * lag
            if 0 <= i < n:
                stages[k](i)


class Rot:
    def __init__(self, nc, stack, name, shape, dtype, n):
        self.t = []
        for i in range(n):
            t = stack.enter_context(nc.sbuf_tensor(f"sb_{name}{i}_{_uid()}", shape, dtype))
            self.t.append((t, Buf(f"{name}{i}")))
        self.i = 0

    def get(self):
        r = self.t[self.i % len(self.t)]
        self.i += 1
        return r


def build_program(debug=False, nlayers=DEPTH, only="bacrf"):
    nc = bass.Bass("TRN2", target_bir_lowering=False)

    def dram(name, shape, kind="ExternalInput", dtype=F32):
        return nc.dram_tensor(name, list(shape), dtype, kind=kind).ap()

    x_in = dram("x", [S, D])
    out_d = dram("out", [S, D], kind="ExternalOutput")
    xs_d = dram("xs", [S, D], kind="Internal")
    w_in = dram("w_in", [DEPTH, D, D_IN])
    gvec = dram("gvec", [DEPTH, 6, D])
    pvec_d = dram("pvec", [DEPTH, 128, KC * 8])
    gws_d = dram("gmlp_ws", [DEPTH, 4, 128, 128])
    gbs_d = dram("gmlp_bs", [DEPTH, 1, 512])
    peT_d = dram("peT", [DEPTH, 2, 64, 32])
    w1_d = dram("nsa_w1", [DEPTH, 2, 2048, 256])
    w2_d = dram("nsa_w2", [DEPTH, 2, 256, 64])
    rbd_d = dram("rnn_bd", [DEPTH, 2, KC, 128, 128])
    wbr_d = dram("w_br", [DEPTH, 3, D, D])
    wo_d = dram("w_o", [DEPTH, D, D])
    wfi_d = dram("w_ffn_in", [DEPTH, D, 2 * DFF])
    wfo_d = dram("w_ffn_out", [DEPTH, DFF, D])
    c_sq = dram("c_sq", [4, 128, 128])
    c_valid = dram("c_valid", [128, S])
    c_E = dram("c_E", [32, S])
    c_msk = dram("c_msk", [3, 128, NT * 32])
    c_ov = dram("c_ov", [128, 33])
    dbg = {}
    if debug:
        for nm in ("ybT", "yaT", "ycT"):
            dbg[nm] = dram("dbg_" + nm, [128, KC * S], kind="ExternalOutput", dtype=BF16)
        dbg["x1"] = dram("dbg_x1", [S, D], kind="ExternalOutput")
        dbg["x2"] = dram("dbg_x2", [S, D], kind="ExternalOutput")

    with ExitStack() as top:
        sc = Sched(nc, top)

        def sbt(stack, name, shape, dtype):
            return stack.enter_context(nc.sbuf_tensor(f"sb_{name}_{_uid()}", list(shape), dtype))

        psum = [top.enter_context(nc.psum_tensor(f"ps{i}", [128, 512], F32)) for i in range(8)]
        psB = [Buf(f"ps{i}") for i in range(8)]

        class PsPool:
            def __init__(self, banks):
                self.banks, self.i = list(banks), 0

            def get(self):
                b = self.banks[self.i % len(self.banks)]
                self.i += 1
                return psum[b], psB[b]

        hT = sbt(top, "hT", [128, KC, S], BF16); hTB = [Buf(f"hT{q}") for q in range(NT)]
        mg = sbt(top, "merged", [128, KC, S], BF16); mgB = [Buf(f"mg{q}") for q in range(NQ)]
        identb = sbt(top, "identb", [128, 128], BF16)
        tril = sbt(top, "tril", [128, 128], F32)
        negcaus = sbt(top, "negcaus", [128, 128], BF16)
        neganti = sbt(top, "neganti", [128, 128], BF16)
        validb = sbt(top, "validb", [128, S], BF16)
        keepm = sbt(top, "keepm", [128, NT * 32], F32)
        addm = sbt(top, "addm", [128, NT * 32], F32)
        lem = sbt(top, "lem", [128, NT * 32], F32)
        ovc = sbt(top, "ovc", [128, 33], BF16)
        onesr = sbt(top, "onesr", [1, 128], BF16)
        kaug = sbt(top, "kaug", [96, S], BF16); kaugB = Buf("kaug")
        pvec = sbt(top, "pvec", [128, DEPTH, KC * 8], F32)
        constB = Buf("const")
        stg = Rot(nc, top, "stg", [128, 2048], F32, 2)
        gbc = Rot(nc, top, "gbc", [128, D], F32, 2)
        small = Rot(nc, top, "small", [128, 8], F32, 12)

        with ExitStack() as ph:
            t_sq = sbt(ph, "t_sq", [128, 3, 128], F32); t_sqB = Buf("t_sq")
            t_val = sbt(ph, "t_val", [128, S], F32); t_valB = Buf("t_val")
            t_E = sbt(ph, "t_E", [96, S], F32); t_EB = Buf("t_E")
            t_ov = sbt(ph, "t_ov", [128, 33], F32); t_ovB = Buf("t_ov")
            for j, src in enumerate((0, 2, 3)):
                sc.dma("sp", t_sq[:, j, :], c_sq[src], [], [t_sqB], t_sqB)
            sc.dma("sp", tril[:], c_sq[1], [], [constB], constB)
            sc.dma("sp", t_val[:], c_valid[:, :], [], [t_valB], t_valB)
            sc.dma("sp", t_E[64:96, :], c_E[:, :], [], [t_EB], t_EB)
            sc.dma("sp", t_ov[:], c_ov[:, :], [], [t_ovB], t_ovB)
            sc.dma("sp", keepm[:], c_msk[0], [], [constB], constB)
            sc.dma("sp", addm[:], c_msk[1], [], [constB], constB)
            sc.dma("sp", lem[:], c_msk[2], [], [constB], constB)
            for l in range(DEPTH):
                sc.dma("sp", pvec[:, l, :], pvec_d[l], [], [constB], constB)
            sc.op("dve", lambda e: e.tensor_copy(out=identb[:], in_=t_sq[:, 0, :]), [t_sqB], [constB])
            sc.op("dve", lambda e: e.tensor_copy(out=negcaus[:], in_=t_sq[:, 1, :]), [t_sqB], [constB])
            sc.op("dve", lambda e: e.tensor_copy(out=neganti[:], in_=t_sq[:, 2, :]), [t_sqB], [constB])
            sc.op("dve", lambda e: e.tensor_copy(out=validb[:], in_=t_val[:]), [t_valB], [constB])
            sc.op("act", lambda e: e.copy(out=kaug[64:96, :], in_=t_E[64:96, :]), [t_EB], [kaugB])
            sc.op("dve", lambda e: e.tensor_copy(out=ovc[:], in_=t_ov[:]), [t_ovB], [constB])
            sc.op("dve", lambda e: e.memset(onesr[:], 1.0), [], [constB])
            sc.barrier()

        def load_w(dst, dstB, src, kk, nn, cast_eng="pool", parts=128):
            assert kk * nn <= 2048
            st, stB = stg.get()
            sv = st[0:parts, 0:kk * nn].rearrange("p (k n) -> p k n", k=kk)
            sc.dma("sp", sv, src, [], [stB], stB)
            sc.op(cast_eng, lambda e: e.tensor_copy(out=dst, in_=sv), [stB], [dstB])

        def wview(w2d, r0, nrows, c0, ncols):
            return w2d[r0:r0 + nrows, c0:c0 + ncols].rearrange("(k p) n -> p k n", p=128)

        def load_gbc(l, idx):
            t, tB = gbc.get()
            sc.dma("sp", t[:], gvec[l, idx:idx + 1, :].partition_broadcast(128), [], [tB], tB)
            return t, tB

        def mm(out, lhsT, rhs, start, stop, reads, wB):
            sc.op("pe", lambda e: e.matmul(out, lhsT=lhsT, rhs=rhs, start=start, stop=stop),
                  reads, [wB], inc=stop)

        def proj_fm(ps, psb, w_ap, wB, Q, M=128, src=None, srcB=None):
            src = hT if src is None else src
            for k in range(KC):
                rb = [wB] + (hTB[4 * Q:4 * Q + 4] if srcB is None else srcB)
                mm(ps[0:M, :], w_ap[:, k, :], src[:, k, Q * 512:(Q + 1) * 512], k == 0, k == KC - 1, rb, psb)

        def rstd_from_ss(ss_ap, ssB, n):
            sc.op("act", lambda e: e.activation(out=ss_ap, in_=ss_ap, func=AF.Sqrt, scale=1.0 / n, bias=EPS), [ssB], [ssB])
            sc.op("dve", lambda e: e.reciprocal(out=ss_ap, in_=ss_ap), [ssB], [ssB])

        def norm_a(xt, xtB, g_t, gB, hb_rot, junk_rot):
            sm, smB = small.get()
            jk, jkB = junk_rot.get()
            sc.op("act", lambda e: e.activation(out=jk[:], in_=xt, func=AF.Square, accum_out=sm[:, 0:1]), [xtB], [jkB, smB])
            rstd_from_ss(sm[:, 0:1], smB, D)
            hb, hbB = hb_rot.get()
            sc.op("dve", lambda e: e.scalar_tensor_tensor(out=hb[:], in0=xt, scalar=sm[:, 0:1], in1=g_t[:], op0=ALU.mult, op1=ALU.mult),
                  [xtB, smB, gB], [hbB])
            return hb, hbB

        def norm_b(hb, hbB, i, pspool):
            ps, psb = pspool.get()
            pv = ps[:].bitcast(BF16)
            for c in range(KC):
                sc.op("pe", lambda e: e.transpose(out=pv[:, c * 128:(c + 1) * 128], in_=hb[:, c * 128:(c + 1) * 128], identity=identb[:]),
                      [hbB, constB], [psb], inc=(c == KC - 1))
            sc.op("act", lambda e: e.copy(out=hT[:, :, i * 128:(i + 1) * 128], in_=pv.rearrange("p (c t) -> p c t", c=KC)), [psb], [hTB[i]])

        def norm_tile_to_hT(xt, xtB, g_t, gB, i, pspool, hb_rot, junk_rot):
            hb, hbB = norm_a(xt, xtB, g_t, gB, hb_rot, junk_rot)
            norm_b(hb, hbB, i, pspool)

        def tap(name, src_ap, srcB, shape3=True):
            if debug and name in dbg:
                d = dbg[name]
                if shape3:
                    d = d.rearrange("p (c t) -> p c t", c=KC)
                sc.dma("sp", d, src_ap, [srcB], [], srcB)

        xsB = [Buf(f"xs{i}") for i in range(NT)]

        def phase_n0():
            with ExitStack() as ph:
                xt_rot = Rot(nc, ph, "n0x", [128, D], F32, 6)
                hb_rot = Rot(nc, ph, "n0h", [128, D], BF16, 3)
                jk_rot = Rot(nc, ph, "n0j", [128, D], BF16, 2)
                pp = PsPool(range(8))
                g_t, gB = load_gbc(0, 0)
                nctx = {}

                def n_a(i):
                    xt, xtB = xt_rot.get()
                    sc.dma("sp", xt[:], x_in[i * 128:(i + 1) * 128, :], [], [xtB], xtB)
                    nctx[i] = (xt, xtB)

                def n_a2(i):
                    pass

                def n_b(i):
                    xt, xtB = nctx[i]
                    nctx[i] = norm_a(xt[:], xtB, g_t, gB, hb_rot, jk_rot)

                def n_c(i):
                    hb, hbB = nctx.pop(i)
                    norm_b(hb, hbB, i, pp)
                skew(NT, [n_a, n_a2, n_b, n_c])
                sc.barrier()

        def merge_branch(l, k, yT, yTB, first):
            with ExitStack() as ph:
                wrot = Rot(nc, ph, "mw", [128, KC, 256], BF16, 4)
                sg_rot = Rot(nc, ph, "msg", [128, 512], F32, 2)
                tmp_rot = Rot(nc, ph, "mtmp", [128, 512], BF16, 2)
                pp = PsPool(range(8))
                def mload(cp):
                    wg, wgB = wrot.get()
                    load_w(wg[:], wgB, wview(w_in[l], 0, D, OFF_MG + k * D + cp * 256, 256), KC, 256, cast_eng="dve")
                    wb, wbB = wrot.get()
                    load_w(wb[:], wbB, wview(wbr_d[l, k], 0, D, cp * 256, 256), KC, 256, cast_eng="dve")
                    return wg, wgB, wb, wbB
                nxt = mload(0)
                for cp in range(4):
                    wg, wgB, wb, wbB = nxt
                    if cp + 1 < 4:
                        nxt = mload(cp + 1)
                    for cc in range(2):
                        co = cp * 2 + cc
                        for Q in range(NQ):
                            ps, psb = pp.get()
                            proj_fm(ps, psb, wg[:, :, cc * 128:(cc + 1) * 128], wgB, Q)
                            sg, sgB = sg_rot.get()
                            sc.op("act", lambda e: e.activation(out=sg[:], in_=ps[:], func=AF.Sigmoid), [psb], [sgB])
                            ps2, ps2b = pp.get()
                            proj_fm(ps2, ps2b, wb[:, :, cc * 128:(cc + 1) * 128], wbB, Q, src=yT, srcB=yTB(Q))
                            dst = mg[:, co, Q * 512:(Q + 1) * 512]
                            if first:
                                sc.op("dve", lambda e: e.tensor_tensor(out=dst, in0=ps2[:], in1=sg[:], op=ALU.mult), [ps2b, sgB], [mgB[Q]])
                            else:
                                tm, tmB = tmp_rot.get()
                                sc.op("dve", lambda e: e.tensor_tensor(out=tm[:], in0=ps2[:], in1=sg[:], op=ALU.mult), [ps2b, sgB], [tmB])
                                sc.op("pool", lambda e: e.tensor_tensor(out=dst, in0=dst, in1=tm[:], op=ALU.add), [tmB, mgB[Q]], [mgB[Q]])
                sc.barrier()

        def phase_c(l, first):
            L = S // 2
            with ExitStack() as ph:
                ycT = sbt(ph, "ycT", [128, KC, S], BF16); ycB = [Buf(f"ycT{q}") for q in range(NQ)]
                ph_out = ph
                ph = ExitStack()
                wrot = Rot(nc, ph, "cw", [128, KC, 256], BF16, 4)
                bdrot = Rot(nc, ph, "cbd", [128, 1, 128], BF16, 4)
                sets = []
                for si in range(2):
                    d = {}
                    for nm, shp, dt_ in (("xr", [128, L + 3], F32), ("xc", [128, L], F32), ("r", [128, L], F32), ("i", [128, L], F32),
                                         ("xb", [128, L], BF16), ("g", [128, L], BF16), ("cxr", [128, 4], F32), ("ch", [128, 2], F32)):
                        d[nm] = (sbt(ph, f"c{nm}{si}", shp, dt_), Buf(f"c{nm}{si}"))
                    sets.append(d)
                coef = sbt(ph, "coef", [128, KC], F32); coefB = Buf("coef")
                coef2 = sbt(ph, "coef2", [128, KC], F32)
                pp = PsPool(range(8))
                pv = pvec[:, l, :].rearrange("p (c k) -> p c k", k=8)
                sc.op("act", lambda e: e.activation(out=coef[:], in_=pv[:, :, 7], func=AF.Exp, scale=-1.0), [constB], [coefB])
                sc.op("act", lambda e: e.activation(out=coef[:], in_=coef[:], func=AF.Ln, bias=1.0), [coefB], [coefB])
                sc.op("dve", lambda e: e.tensor_scalar(out=coef[:], in0=coef[:], scalar1=-8.0, scalar2=None, op0=ALU.mult), [coefB], [coefB])
                sc.op("dve", lambda e: e.tensor_scalar(out=coef2[:], in0=coef[:], scalar1=2.0, scalar2=None, op0=ALU.mult), [coefB], [coefB])

                def unit(c, hf, st, wx, wxB, wr, wrB, bd):
                    cc = c % 2
                    (xr, xrB), (xc, xcB), (tr, rB), (ti, iB) = st["xr"], st["xc"], st["r"], st["i"]
                    (xb, xbB), (tg, gB_), (cxr, cxrB), (ch, chB) = st["xb"], st["g"], st["cxr"], st["ch"]
                    bda, bdaB, bdx, bdxB = bd
                    if hf == 0:
                        sc.op("dve", lambda e: e.memset(xr[:, 0:3], 0.0), [], [xrB])
                    else:
                        sc.op("dve", lambda e: e.tensor_copy(out=xr[:, 0:3], in_=cxr[:, 0:3]), [cxrB], [xrB])
                    for qq in range(2):
                        Q = 2 * hf + qq
                        ps, psb = pp.get()
                        proj_fm(ps, psb, wx[:, :, cc * 128:(cc + 1) * 128], wxB, Q)
                        sc.op("act", lambda e: e.copy(out=xr[:, 3 + qq * 512:3 + (qq + 1) * 512], in_=ps[:]), [psb], [xrB])
                    yield
                    for qq in range(2):
                        Q = 2 * hf + qq
                        ps, psb = pp.get()
                        proj_fm(ps, psb, wr[:, :, cc * 128:(cc + 1) * 128], wrB, Q)
                        sc.op("act", lambda e: e.activation(out=tg[:, qq * 512:(qq + 1) * 512], in_=ps[:], func=AF.Gelu_apprx_tanh), [psb], [gB_])
                    yield
                    sc.op("dve", lambda e: e.tensor_scalar(out=xc[:], in0=xr[:, 3:3 + L], scalar1=pv[:, c, 3:4], scalar2=pv[:, c, 4:5],
                                                           op0=ALU.mult, op1=ALU.add), [xrB, constB], [xcB])
                    yield
                    for j in range(3):
                        sc.op("dve", lambda e: e.scalar_tensor_tensor(out=xc[:], in0=xr[:, j:j + L], scalar=pv[:, c, j:j + 1], in1=xc[:],
                                                                      op0=ALU.mult, op1=ALU.add), [xrB, xcB, constB], [xcB])
                        yield
                    if hf == 0:
                        sc.op("dve", lambda e: e.tensor_copy(out=cxr[:, 0:3], in_=xr[:, L:L + 3]), [xrB], [cxrB])
                    sc.op("act", lambda e: e.copy(out=xb[:], in_=xc[:]), [xcB], [xbB])
                    yield
                    for qq in range(2):
                        ps, psb = pp.get()
                        mm(ps[:], bda[:, 0, :], xb[:, qq * 512:(qq + 1) * 512], True, True, [bdaB, xbB], psb)
                        sc.op("act", lambda e: e.activation(out=tr[:, qq * 512:(qq + 1) * 512], in_=ps[:], func=AF.Sigmoid, bias=pv[:, c, 5:6]),
                              [psb, constB], [rB])
                        ps, psb = pp.get()
                        mm(ps[:], bdx[:, 0, :], xb[:, qq * 512:(qq + 1) * 512], True, True, [bdxB, xbB], psb)
                        sc.op("act", lambda e: e.activation(out=ti[:, qq * 512:(qq + 1) * 512], in_=ps[:], func=AF.Sigmoid, bias=pv[:, c, 6:7]),
                              [psb, constB], [iB])
                    yield
                    sc.op("dve", lambda e: e.tensor_tensor(out=ti[:], in0=ti[:], in1=xc[:], op=ALU.mult), [iB, xcB], [iB])
                    yield
                    sc.op("act", lambda e: e.activation(out=xc[:], in_=tr[:], func=AF.Exp, scale=coef2[:, c:c + 1]), [rB, coefB, iB], [xcB])
                    sc.op("act", lambda e: e.activation(out=tr[:], in_=tr[:], func=AF.Exp, scale=coef[:, c:c + 1]), [rB, coefB], [rB])
                    yield
                    sc.op("act", lambda e: e.activation(out=xc[:], in_=xc[:], func=AF.Sqrt, scale=-1.0, bias=1.0), [xcB], [xcB])
                    yield
                    sc.op("dve", lambda e: e.tensor_tensor(out=ti[:], in0=ti[:], in1=xc[:], op=ALU.mult), [iB, xcB], [iB])
                    yield
                    if hf == 0:
                        sc.op("dve", lambda e: e.tensor_tensor_scan(out=xc[:], data0=tr[:], data1=ti[:], initial=0.0, op0=ALU.mult, op1=ALU.add),
                              [rB, iB], [xcB])
                        sc.op("dve", lambda e: e.tensor_copy(out=ch[:, 0:1], in_=xc[:, L - 1:L]), [xcB], [chB])
                    else:
                        sc.op("dve", lambda e: e.tensor_tensor_scan(out=xc[:], data0=tr[:], data1=ti[:], initial=ch[:, 0:1], op0=ALU.mult, op1=ALU.add),
                              [rB, iB, chB], [xcB])
                    yield
                    sc.op("dve", lambda e: e.tensor_tensor(out=ycT[:, c, hf * L:(hf + 1) * L], in0=xc[:], in1=tg[:], op=ALU.mult), [xcB, gB_], ycB[2 * hf:2 * hf + 2])

                for cp in range(KC // 2):
                    wx, wxB = wrot.get()
                    load_w(wx[:], wxB, wview(w_in[l], 0, D, OFF_XR + cp * 256, 256), KC, 256)
                    wr, wrB = wrot.get()
                    load_w(wr[:], wrB, wview(w_in[l], 0, D, OFF_RG + cp * 256, 256), KC, 256)
                    bds = []
                    for cc in range(2):
                        c = 2 * cp + cc
                        bda, bdaB = bdrot.get()
                        load_w(bda[:], bdaB, rbd_d[l, 0, c].rearrange("p (k n) -> p k n", k=1), 1, 128)
                        bdx, bdxB = bdrot.get()
                        load_w(bdx[:], bdxB, rbd_d[l, 1, c].rearrange("p (k n) -> p k n", k=1), 1, 128)
                        bds.append((bda, bdaB, bdx, bdxB))
                    for hf in range(2):
                        gens = [unit(2 * cp + cc, hf, sets[cc], wx, wxB, wr, wrB, bds[cc]) for cc in range(2)]
                        alive = True
                        while alive:
                            alive = False
                            for gn in gens:
                                try:
                                    next(gn)
                                    alive = True
                                except StopIteration:
                                    pass
                sc.barrier()
                ph.close()
                tap("ycT", ycT[:], ycB[0])
                merge_branch(l, 2, ycT, lambda Q: [ycB[Q]], first)

        def phase_a(l, first):
            with ExitStack() as ph:
                yaT = sbt(ph, "yaT", [128, KC, S], BF16); yaB = [Buf(f"yaT{q}") for q in range(NQ)]
                ph2 = ExitStack()
                wv = sbt(ph2, "wv", [128, KC, D], BF16); wvB = [Buf(f"wv{j}") for j in range(4)]
                wsT = sbt(ph2, "wsT", [128, 4, 128], BF16); wsTB = Buf("wsT")
                wsf = sbt(ph2, "wsf", [128, 4, 128], F32); wsfB = Buf("wsf")
                wsm = sbt(ph2, "wsm", [128, 4, 128], BF16); wsmB = Buf("wsm")
                bsf = sbt(ph2, "bsf", [1, 512], F32); bsfB = Buf("bsf")
                bsr = sbt(ph2, "bsr", [1, 512], BF16); bsrB = Buf("bsr")
                wrot = Rot(nc, ph2, "aw", [128, KC, 256], BF16, 2)
                vg_rot = Rot(nc, ph2, "avg", [128, D], F32, 4)
                vn_rot = Rot(nc, ph2, "avn", [128, D], BF16, 3)
                st_rot = Rot(nc, ph2, "ast", [128, 16], F32, 5)
                pp = PsPool(range(8))
                sc.dma("sp", wsf[:], gws_d[l].rearrange("g t s -> t g s"), [], [wsfB], wsfB)
                sc.dma("sp", bsf[:], gbs_d[l], [], [bsfB], bsfB)
                for g in range(4):
                    sc.op("dve", lambda e: e.tensor_tensor(out=wsm[:, g, :], in0=wsf[:, g, :], in1=tril[:], op=ALU.mult), [wsfB, constB], [wsmB])
                ps, psb = pp.get()
                pvw = ps[:].bitcast(BF16)
                for g in range(4):
                    sc.op("pe", lambda e: e.transpose(out=pvw[:, g * 128:(g + 1) * 128], in_=wsm[:, g, :], identity=identb[:]),
                          [wsmB, constB], [psb], inc=(g == 3))
                sc.op("act", lambda e: e.copy(out=wsT[:], in_=pvw[:, 0:512].rearrange("p (g t) -> p g t", g=4)), [psb], [wsTB])
                sc.op("dve", lambda e: e.tensor_copy(out=bsr[:], in_=bsf[:]), [bsfB], [bsrB])
                def aload(cp):
                    wu, wuB = wrot.get()
                    load_w(wu[:], wuB, wview(w_in[l], 0, D, OFF_U + cp * 256, 256), KC, 256)
                    return wu, wuB
                nxt = aload(0)
                for cp in range(4):
                    wu, wuB = nxt
                    if cp + 1 < 4:
                        nxt = aload(cp + 1)
                    for cc in range(2):
                        c = cp * 2 + cc
                        for Q in range(NQ):
                            ps, psb = pp.get()
                            proj_fm(ps, psb, wu[:, :, cc * 128:(cc + 1) * 128], wuB, Q)
                            sc.op("act", lambda e: e.activation(out=yaT[:, c, Q * 512:(Q + 1) * 512], in_=ps[:], func=AF.Gelu_apprx_tanh), [psb], [yaB[Q]])
                for j in range(4):
                    load_w(wv[:, :, j * 256:(j + 1) * 256], wvB[j], wview(w_in[l], 0, D, OFF_V + j * 256, 256), KC, 256)
                lng, lngB = load_gbc(l, 4)
                lnb, lnbB = load_gbc(l, 5)
                ctx = {}

                def a_s1(i):
                    vg, vgB = vg_rot.get()
                    for hf in range(2):
                        ps, psb = pp.get()
                        for k in range(KC):
                            mm(ps[:], hT[:, k, i * 128:(i + 1) * 128], wv[:, k, hf * 512:(hf + 1) * 512], k == 0, k == KC - 1,
                               [hTB[i], wvB[2 * hf], wvB[2 * hf + 1]], psb)
                        sc.op("act", lambda e: e.activation(out=vg[:, hf * 512:(hf + 1) * 512], in_=ps[:], func=AF.Gelu_apprx_tanh), [psb], [vgB])
                    ctx[i] = dict(vg=(vg, vgB))

                def a_s2(i):
                    vg, vgB = ctx[i]["vg"]
                    stt, sttB = st_rot.get()
                    for hf in range(2):
                        sc.op("dve", lambda e: e.bn_stats(out=stt[:, hf * 6:(hf + 1) * 6], in_=vg[:, hf * 512:(hf + 1) * 512]), [vgB], [sttB])
                    sc.op("dve", lambda e: e.bn_aggr(out=stt[:, 12:14], in_=stt[:, 0:12]), [sttB], [sttB])
                    sc.op("act", lambda e: e.activation(out=stt[:, 13:14], in_=stt[:, 13:14], func=AF.Sqrt, bias=EPS), [sttB], [sttB])
                    sc.op("dve", lambda e: e.reciprocal(out=stt[:, 13:14], in_=stt[:, 13:14]), [sttB], [sttB])
                    sc.op("dve", lambda e: e.scalar_tensor_tensor(out=vg[:], in0=vg[:], scalar=stt[:, 12:13], in1=lng[:], op0=ALU.subtract, op1=ALU.mult),
                          [vgB, sttB, lngB], [vgB])
                    ctx[i]["stt"] = (stt, sttB)

                def a_s3(i):
                    vg, vgB = ctx[i]["vg"]
                    stt, sttB = ctx[i]["stt"]
                    vn, vnB = vn_rot.get()
                    sc.op("dve", lambda e: e.scalar_tensor_tensor(out=vn[:], in0=vg[:], scalar=stt[:, 13:14], in1=lnb[:], op0=ALU.mult, op1=ALU.add),
                          [vgB, sttB, lnbB], [vnB])
                    ctx[i]["vn"] = (vn, vnB)

                def a_s4(i):
                    vn, vnB = ctx[i]["vn"]
                    for hf in range(2):
                        ps, psb = pp.get()
                        for cc in range(4):
                            c = hf * 4 + cc
                            g = c // 2
                            mm(ps[:, cc * 128:(cc + 1) * 128], vn[:, c * 128:(c + 1) * 128], wsT[:, g, :], True, False, [vnB, wsTB], psb)
                            mm(ps[:, cc * 128:(cc + 1) * 128], onesr[0:1, :], bsr[0:1, g * 128:(g + 1) * 128], False, True, [constB, bsrB], psb)
                        dst = yaT[:, hf * 4:hf * 4 + 4, i * 128:(i + 1) * 128]
                        sc.op("dve", lambda e: e.tensor_tensor(out=dst, in0=dst, in1=ps[:].rearrange("p (c t) -> p c t", c=4), op=ALU.mult),
                              [psb, yaB[i // 4]], [yaB[i // 4]])
                    del ctx[i]
                skew(NT, [a_s1, a_s2, a_s3, a_s4])
                sc.barrier()
                ph2.close()
                tap("yaT", yaT[:], yaB[0])
                merge_branch(l, 0, yaT, lambda Q: [yaB[Q]], first)

        def phase_b(l, first):
            with ExitStack() as ph:
                ybT = sbt(ph, "ybT", [128, KC, S], BF16); ybTB = [Buf(f"ybT{q}") for q in range(NQ)]
                ph2 = ExitStack()
                vslc = sbt(ph2, "vslc", [128, NT, 4, 65], BF16); vslcB = Buf("vslc")
                vwin = sbt(ph2, "vwin", [128, NT, 4, 65], BF16); vwinB = Buf("vwin")
                gates = sbt(ph2, "gates", [128, NT, 48], F32); gatesB = Buf("gates")
                kcT = sbt(ph2, "kcT", [96, 4, 128], BF16); kcTB = Buf("kcT")
                vca = sbt(ph2, "vca", [128, 4, 97], BF16); vcaB = Buf("vca")
                wrot = Rot(nc, ph2, "bw", [128, KC, 256], BF16, 2)
                pp = PsPool(range(2, 8))
                acc = PsPool([0, 1])
                sc.op("dve", lambda e: e.memset(vslc[:, :, :, 64:65], 1.0), [], [vslcB])
                sc.op("dve", lambda e: e.memset(vwin[:, :, :, 64:65], 1.0), [], [vwinB])
                sc.op("dve", lambda e: e.memset(vca[:], 0.0), [], [vcaB])
                sc.op("dve", lambda e: e.memset(kcT[:], 0.0), [], [kcTB])
                for g in range(4):
                    sc.op("dve", lambda e: e.tensor_copy(out=vca[:, g, 64:97], in_=ovc[:]), [constB], [vcaB])
                if BSTOP >= 1:
                    wng, wngB = wrot.get()
                    load_w(wng[:, :, 0:48], wngB, wview(w_in[l], 0, D, OFF_NG, 48), KC, 48)
                    for q in range(NQ):
                        ps, psb = pp.get()
                        for j in range(4):
                            i = 4 * q + j
                            for k in range(KC):
                                mm(ps[:, j * 48:(j + 1) * 48], hT[:, k, i * 128:(i + 1) * 128], wng[:, k, 0:48], k == 0, k == KC - 1, [hTB[i], wngB], psb)
                        sc.op("act", lambda e: e.activation(out=gates[:, 4 * q:4 * q + 4, :], in_=ps[:, 0:192].rearrange("p (j c) -> p j c", j=4), func=AF.Sigmoid),
                              [psb], [gatesB])
                if BSTOP >= 2:
                    for vt, vtB, blk in ((vslc, vslcB, 3), (vwin, vwinB, 5)):
                        wvv, wvvB = wrot.get()
                        load_w(wvv[:], wvvB, wview(w_in[l], 0, D, OFF_KV + blk * 256, 256), KC, 256)
                        for i in range(NT):
                            ps, psb = pp.get()
                            for k in range(KC):
                                mm(ps[:, 0:256], hT[:, k, i * 128:(i + 1) * 128], wvv[:, k, :], k == 0, k == KC - 1, [hTB[i], wvvB], psb)
                            eng = "act" if i % 2 == 0 else "dve"
                            if eng == "act":
                                sc.op("act", lambda e: e.copy(out=vt[:, i, :, 0:64], in_=ps[:, 0:256].rearrange("p (g d) -> p g d", g=4)), [psb], [vtB])
                            else:
                                sc.op("dve", lambda e: e.tensor_copy(out=vt[:, i, :, 0:64], in_=ps[:, 0:256].rearrange("p (g d) -> p g d", g=4)), [psb], [vtB])
                if BSTOP >= 3:
                    with ExitStack() as ph3:
                        cmpT = sbt(ph3, "cmpT", [64, 4, 16, 128], BF16); cmpTB = Buf("cmpT")
                        w1rot = Rot(nc, ph3, "w1t", [64, 32, 128], BF16, 2)
                        w2t = sbt(ph3, "w2t", [128, 2, 2, 64], BF16); w2B = Buf("w2t")
                        peT = sbt(ph3, "peT", [64, 2, 32], BF16); peTB = Buf("peT")
                        hid = sbt(ph3, "hid", [128, 4, 2, 128], BF16); hidB = Buf("hid")
                        cvec = sbt(ph3, "cvec", [128, 4], F32); cvecB = Buf("cvec")
                        for kv in range(2):
                            load_w(w2t[:, kv, :, :], w2B, w2_d[l, kv].rearrange("(jc p) d -> p jc d", p=128), 2, 64)
                            load_w(peT[:, kv:kv + 1, :], peTB, peT_d[l, kv].rearrange("d (a n) -> d a n", a=1), 1, 32, parts=64)
                        for kv in range(2):
                            wkc, wkcB = wrot.get()
                            load_w(wkc[:], wkcB, wview(w_in[l], 0, D, OFF_KV + kv * 256, 256), KC, 256)
                            for gp in range(2):
                                for Q in range(NQ):
                                    ps, psb = pp.get()
                                    proj_fm(ps, psb, wkc[:, :, gp * 128:(gp + 1) * 128], wkcB, Q, M=128)
                                    for gb in range(2):
                                        g = 2 * gp + gb
                                        sc.op("act", lambda e: e.copy(out=cmpT[:, g, :, Q * 32:(Q + 1) * 32].rearrange("d r n -> d n r"),
                                                                      in_=ps[gb * 64:(gb + 1) * 64, :].rearrange("d (n r) -> d n r", r=16)), [psb], [cmpTB])
                            w1v = w1_d[l, kv].rearrange("(l d) j -> d l j", d=64)
                            for jc in range(2):
                                w1t, w1B = w1rot.get()
                                for lh in range(2):
                                    load_w(w1t[:, lh * 16:(lh + 1) * 16, :], w1B, w1v[:, lh * 16:(lh + 1) * 16, jc * 128:(jc + 1) * 128], 16, 128, parts=64)
                                ps, psb = pp.get()
                                for li in range(32):
                                    mm(ps[:, 0:1], w1t[:, li, :], peT[:, kv, li:li + 1], li == 0, li == 31, [w1B, peTB], psb)
                                cv = cvec[:, kv * 2 + jc:kv * 2 + jc + 1]
                                sc.op("dve", lambda e: e.tensor_copy(out=cv, in_=ps[:, 0:1]), [psb], [cvecB])
                                for g in range(4):
                                    ps, psb = pp.get()
                                    for li in range(32):
                                        mm(ps[:, 0:NCMP], w1t[:, li, :], cmpT[:, g, li % 16, li // 16:li // 16 + NCMP], li == 0, li == 31, [w1B, cmpTB], psb)
                                    sc.op("act", lambda e: e.activation(out=hid[:, g, jc, 0:NCMP], in_=ps[:, 0:NCMP], func=AF.Gelu_apprx_tanh, bias=cv),
                                          [psb, cvecB], [hidB])
                            for g in range(4):
                                ps, psb = pp.get()
                                if kv == 0:
                                    for jc in range(2):
                                        mm(ps[0:64, 0:NCMP], w2t[:, 0, jc, :], hid[:, g, jc, 0:NCMP], jc == 0, jc == 1, [w2B, hidB], psb)
                                    sc.op("dve", lambda e: e.tensor_copy(out=kcT[0:64, g, 0:NCMP], in_=ps[0:64, 0:NCMP]), [psb], [kcTB])
                                else:
                                    for jc in range(2):
                                        mm(ps[0:NCMP, 0:64], hid[:, g, jc, 0:NCMP], w2t[:, 1, jc, :], jc == 0, jc == 1, [w2B, hidB], psb)
                                    sc.op("dve", lambda e: e.tensor_copy(out=vca[0:NCMP, g, 0:64], in_=ps[0:NCMP, 0:64]), [psb], [vcaB])
                        sc.barrier()
                qaug = [mg[0:96, 4 + hh, :] for hh in range(4)]
                qaB = [Buf(f"qaug{hh}") for hh in range(4)]
                qmB = [Buf(f"qmask{hh}") for hh in range(4)]
                kwinT = sbt(ph2, "kwinT", [96, S], BF16); kwinB = Buf("kwinT")
                sc.op("dve", lambda e: e.memset(kwinT[64:96, :], 0.0), [], [kwinB])
                for hh in range(4):
                    sc.op("dve", lambda e: e.memset(qaug[hh][64:96, :], 0.0), [], [qmB[hh]])
                yb = mg[:, 0:4, :].bitcast(F32).rearrange("p a (b d) -> p (a b) d", d=256); ybB = [Buf(f"yb{q}") for q in range(NQ)]
                imp = sbt(ph2, "imp", [128, NT, 32], F32); impB = [Buf(f"imp{q}") for q in range(NQ)]
                pT_rot = Rot(nc, ph2, "pT", [128, 512], BF16, 6)
                e_rot = Rot(nc, ph2, "eT", [128, 512], BF16, 4)
                impf_rot = Rot(nc, ph2, "impf", [128, 32], F32, 3)
                imps_rot = Rot(nc, ph2, "imps", [128, 32], F32, 3)
                xpad = sbt(ph2, "xpad", [128, NT, 96], BF16); xpadB = [Buf(f"xpad{i}") for i in range(NT)]
                ybb_rot = Rot(nc, ph2, "ybb", [128, 256], BF16, 2)
                wk_rot = Rot(nc, ph2, "bwk", [128, KC, 128], BF16, 2)
                etmp_rot = Rot(nc, ph2, "etmp", [128, 4, 64], F32, 3)
                itmp_rot = Rot(nc, ph2, "itmp", [128, 4, 32], F32, 2)
                sc.op("dve", lambda e: e.memset(xpad[:, :, 0:64], 0.0), [], xpadB)

                def epilogue(ps, psb, Q, hh, h, br, firstbr):
                    W = 97 if br == 0 else 65
                    sm, smB = small.get()
                    p3 = ps[:, 0:4 * W].rearrange("p (j w) -> p j w", j=4)
                    sc.op("dve", lambda e: e.tensor_scalar(out=sm[:, 0:4], in0=p3[:, :, 64], scalar1=1e-30, scalar2=None, op0=ALU.add), [psb], [smB])
                    sc.op("dve", lambda e: e.reciprocal(out=sm[:, 0:4], in_=sm[:, 0:4]), [smB], [smB])
                    sc.op("dve", lambda e: e.tensor_tensor(out=sm[:, 4:8], in0=sm[:, 0:4], in1=gates[:, 4 * Q:4 * Q + 4, h * 3 + br], op=ALU.mult),
                          [smB, gatesB], [smB])
                    dst = yb[:, 4 * Q:4 * Q + 4, hh * 64:(hh + 1) * 64]
                    cb = sm[:, 4:8].unsqueeze(2).to_broadcast([128, 4, 64])
                    if firstbr:
                        sc.op("dve", lambda e: e.tensor_tensor(out=dst, in0=p3[:, :, 0:64], in1=cb, op=ALU.mult), [psb, smB], [ybB[Q]])
                    else:
                        tm, tmB = etmp_rot.get()
                        sc.op("dve", lambda e: e.tensor_tensor(out=tm[:], in0=p3[:, :, 0:64], in1=cb, op=ALU.mult), [psb, smB], [tmB])
                        sc.op("pool", lambda e: e.tensor_tensor(out=dst, in0=dst, in1=tm[:], op=ALU.add), [tmB, ybB[Q]], [ybB[Q]])
                    if br == 0:
                        di = imp[:, 4 * Q:4 * Q + 4, :]
                        rb = sm[:, 0:4].unsqueeze(2).to_broadcast([128, 4, 32])
                        if hh == 0:
                            sc.op("dve", lambda e: e.tensor_tensor(out=di, in0=p3[:, :, 65:97], in1=rb, op=ALU.mult), [psb, smB], [impB[Q]])
                        else:
                            tm, tmB = itmp_rot.get()
                            sc.op("dve", lambda e: e.tensor_tensor(out=tm[:], in0=p3[:, :, 65:97], in1=rb, op=ALU.mult), [psb, smB], [tmB])
                            sc.op("pool", lambda e: e.tensor_tensor(out=di, in0=di, in1=tm[:], op=ALU.add), [tmB, impB[Q]], [impB[Q]])

                def attn_items(kind):
                    items = []
                    for hh in range(4):
                        for Q in range(NQ):
                            kts = list(range(4 * Q + 4)) if kind == "slc" else list(range(max(0, 4 * Q - 4), 4 * Q + 4))
                            for n_, kt in enumerate(kts):
                                r = kt - 4 * Q
                                jlo = max(0, r)
                                jhi = 3 if kind == "slc" else min(3, r + 4)
                                items.append(dict(kind=kind, hh=hh, Q=Q, kt=kt, r=r, jlo=jlo, jhi=jhi, first=(n_ == 0), last=(n_ == len(kts) - 1)))
                    return items

                def stage1(it, g):
                    hh, Q, kt, r, jlo, jhi = it["hh"], it["Q"], it["kt"], it["r"], it["jlo"], it["jhi"]
                    N = 128 * (jhi - jlo + 1)
                    c0 = Q * 512 + 128 * jlo
                    ps, psb = pp.get()
                    diag = r >= 0
                    anti = it["kind"] == "win" and r <= -1
                    if it["kind"] == "slc":
                        mm(ps[:, 0:N], kaug[0:96, kt * 128:(kt + 1) * 128], qaug[hh][0:96, c0:c0 + N], True, not (diag or anti), [kaugB, qaB[hh], qmB[hh]], psb)
                    else:
                        mm(ps[:, 0:N], kwinT[0:96, kt * 128:(kt + 1) * 128], qaug[hh][0:96, c0:c0 + N], True, not (diag or anti), [kwinB, qaB[hh], qmB[hh]], psb)
                    if diag:
                        mm(ps[:, 0:128], identb[:], negcaus[:], False, True, [constB], psb)
                    if anti:
                        mm(ps[:, N - 128:N], identb[:], neganti[:], False, True, [constB], psb)
                    pT, pTB = pT_rot.get()
                    sc.op("act", lambda e: e.activation(out=pT[:, 0:N], in_=ps[:, 0:N], func=AF.Exp, scale=0.125), [psb], [pTB])
                    it["pT"] = (pT, pTB)

                def stage2(it, g, state, post=None):
                    hh, Q, kt, jlo, jhi = it["hh"], it["Q"], it["kt"], it["jlo"], it["jhi"]
                    if it["first"]:
                        state["po"] = acc.get()
                    po, pob = state["po"]
                    pT, pTB = it["pT"]
                    vt, vtB = (vslc, vslcB) if it["kind"] == "slc" else (vwin, vwinB)
                    for j in range(jlo, jhi + 1):
                        mm(po[:, j * 65:(j + 1) * 65], pT[:, (j - jlo) * 128:(j - jlo + 1) * 128], vt[:, kt, g, :],
                           it["first"] and j == jlo, it["last"] and j == jhi, [pTB, vtB], pob)
                    if it["last"]:
                        epilogue(po, pob, Q, hh, 4 * g + hh, 1 if it["kind"] == "slc" else 2, False)
                        if post is not None:
                            post(hh * NQ + Q)

                def run_items(items, g, look=3, post=None):
                    state = {}
                    for n_ in range(min(look, len(items))):
                        stage1(items[n_], g)
                    for n_, it in enumerate(items):
                        if n_ + look < len(items):
                            stage1(items[n_ + look], g)
                        stage2(it, g, state, post)

                for g in range(4):
                    wq, wqB = wrot.get()
                    load_w(wq[:], wqB, wview(w_in[l], 0, D, OFF_Q + g * 256, 256), KC, 256)
                    for hp in range(2):
                        for Q in range(NQ):
                            ps, psb = pp.get()
                            proj_fm(ps, psb, wq[:, :, hp * 128:(hp + 1) * 128], wqB, Q, M=128)
                            for hb in range(2):
                                hh = 2 * hp + hb
                                if (hp + Q) % 2 == 0:
                                    sc.op("act", lambda e: e.copy(out=qaug[hh][0:64, Q * 512:(Q + 1) * 512], in_=ps[hb * 64:(hb + 1) * 64, :]), [psb], [qaB[hh]])
                                else:
                                    sc.op("dve", lambda e: e.tensor_copy(out=qaug[hh][0:64, Q * 512:(Q + 1) * 512], in_=ps[hb * 64:(hb + 1) * 64, :]), [psb], [qaB[hh]])
                    wk, wkB = wk_rot.get()
                    load_w(wk[:, :, 0:64], wkB, wview(w_in[l], 0, D, OFF_KV + 2 * 256 + g * 64, 64), KC, 64)
                    load_w(wk[:, :, 64:128], wkB, wview(w_in[l], 0, D, OFF_KV + 4 * 256 + g * 64, 64), KC, 64)
                    for Q in range(NQ):
                        ps, psb = pp.get()
                        proj_fm(ps, psb, wk[:, :, :], wkB, Q, M=128)
                        sc.op("act", lambda e: e.copy(out=kaug[0:64, Q * 512:(Q + 1) * 512], in_=ps[0:64, :]), [psb], [kaugB])
                        sc.op("act", lambda e: e.copy(out=kwinT[0:64, Q * 512:(Q + 1) * 512], in_=ps[64:128, :]), [psb], [kwinB])
                    cctx = {}

                    def cmp_s1(u):
                        hh, Q = divmod(u, NQ)
                        ps, psb = pp.get()
                        mm(ps[:, :], kcT[:, g, :], qaug[hh][0:96, Q * 512:(Q + 1) * 512], True, False, [kcTB, qaB[hh], qmB[hh]], psb)
                        mm(ps[:, :], identb[:], validb[:, Q * 512:(Q + 1) * 512], False, True, [constB], psb)
                        et, etB = e_rot.get()
                        sc.op("act", lambda e: e.activation(out=et[:, :], in_=ps[:, :], func=AF.Exp, scale=0.125), [psb], [etB])
                        cctx[u] = (et, etB)

                    def cmp_s2(u):
                        hh, Q = divmod(u, NQ)
                        et, etB = cctx.pop(u)
                        ps2, ps2b = pp.get()
                        for j in range(4):
                            mm(ps2[:, j * 97:(j + 1) * 97], et[:, j * 128:(j + 1) * 128], vca[:, g, :], True, True, [etB, vcaB], ps2b)
                        epilogue(ps2, ps2b, Q, hh, 4 * g + hh, 0, True)
                    skew(4 * NQ, [cmp_s1, cmp_s2], lag=2)
                    def sel_tile(i):
                        ip, ipB = impf_rot.get()
                        ip2, ip2B = imps_rot.get()
                        sm, smB = small.get()
                        sm2, sm2B = small.get()
                        sc.op("dve", lambda e: e.tensor_tensor(out=ip[:], in0=imp[:, i, :], in1=keepm[:, i * 32:(i + 1) * 32], op=ALU.mult), [impB[i // 4], constB], [ipB])
                        sc.op("dve", lambda e: e.tensor_tensor(out=ip[:], in0=ip[:], in1=addm[:, i * 32:(i + 1) * 32], op=ALU.add), [ipB, constB], [ipB])
                        sc.op("dve", lambda e: e.max(out=sm[:], in_=ip[:]), [ipB], [smB])
                        sc.op("dve", lambda e: e.match_replace(out=ip2[:], in_to_replace=sm[:], in_values=ip[:], imm_value=-3.0e38), [ipB, smB], [ip2B])
                        sc.op("dve", lambda e: e.max(out=sm2[:], in_=ip2[:]), [ip2B], [sm2B])
                        sc.op("dve", lambda e: e.scalar_tensor_tensor(out=ip[:], in0=ip[:], scalar=sm2[:, 7:8], in1=lem[:, i * 32:(i + 1) * 32], op0=ALU.is_ge, op1=ALU.mult),
                              [ipB, sm2B, constB], [ipB])
                        sc.op("dve", lambda e: e.tensor_scalar(out=xpad[:, i, 64:96], in0=ip[:], scalar1=NEGM, scalar2=-NEGM, op0=ALU.mult, op1=ALU.add), [ipB], [xpadB[i]])
                    run_items(attn_items("win"), g, post=sel_tile)
                    for Q in range(NQ):
                        ps, psb = pp.get()
                        for j in range(4):
                            i = 4 * Q + j
                            mm(ps[0:96, j * 128:(j + 1) * 128], xpad[:, i, :], identb[:], True, True, [xpadB[i], constB], psb)
                        for hh in range(4):
                            sc.op("dve", lambda e: e.tensor_copy(out=qaug[hh][64:96, Q * 512:(Q + 1) * 512], in_=ps[64:96, :]), [psb], [qmB[hh]])
                    run_items(attn_items("slc"), g)
                    for Q in range(NQ):
                        ps, psb = pp.get()
                        pvb = ps[:].bitcast(BF16)
                        for j in range(4):
                            i = 4 * Q + j
                            ybb, ybbB = ybb_rot.get()
                            sc.op("act", lambda e: e.copy(out=ybb[:], in_=yb[:, i, :]), [ybB[Q]], [ybbB])
                            for cc in range(2):
                                sc.op("pe", lambda e: e.transpose(out=pvb[:, cc * 512 + j * 128:cc * 512 + (j + 1) * 128], in_=ybb[:, cc * 128:(cc + 1) * 128], identity=identb[:]),
                                      [ybbB, constB], [psb], inc=(cc == 1))
                        sc.op("dve", lambda e: e.tensor_copy(out=ybT[:, 2 * g:2 * g + 2, Q * 512:(Q + 1) * 512], in_=pvb.rearrange("p (c n) -> p c n", c=2)), [psb], [ybTB[Q]])
                sc.barrier()
                ph2.close()
                tap("ybT", ybT[:], ybTB[0])
                merge_branch(l, 1, ybT, lambda Q: [ybTB[Q]], first)

        def resid_load(i, xsrc, src_is_scratch, rots):
            xt, xtB = rots[0].get()
            rd = [xsB[i]] if src_is_scratch else []
            sc.dma("sp", xt[:], xsrc[i * 128:(i + 1) * 128, :], rd, [xtB], xtB)
            return xt, xtB

        def resid_tile(i, halves, xload, xdst, dst_is_scratch, g_post, g_postB, g_next, g_nextB, rots, pp, tapname=None):
            xt_rot, tmp_rot, xn_rot, hb_rot, jk_rot = rots
            xt, xtB = xload
            sm, smB = small.get()
            jk, jkB = jk_rot.get()
            for hf, (ps, psb) in enumerate(halves):
                sc.op("act", lambda e: e.activation(out=jk[:, hf * 512:(hf + 1) * 512], in_=ps[:], func=AF.Square, accum_out=sm[:, hf:hf + 1]), [psb], [jkB, smB])
            sc.op("dve", lambda e: e.tensor_tensor(out=sm[:, 2:3], in0=sm[:, 0:1], in1=sm[:, 1:2], op=ALU.add), [smB], [smB])
            rstd_from_ss(sm[:, 2:3], smB, D)
            tm, tmB = tmp_rot.get()
            for hf, (ps, psb) in enumerate(halves):
                sc.op("dve", lambda e: e.scalar_tensor_tensor(out=tm[:, hf * 512:(hf + 1) * 512], in0=ps[:], scalar=sm[:, 2:3], in1=g_post[:, hf * 512:(hf + 1) * 512],
                                                              op0=ALU.mult, op1=ALU.mult), [psb, smB, g_postB], [tmB])
            xn, xnB = xn_rot.get()
            sc.op("pool", lambda e: e.tensor_tensor(out=xn[:], in0=tm[:], in1=xt[:], op=ALU.add), [tmB, xtB], [xnB])
            wr = [xsB[i]] if dst_is_scratch else []
            sc.dma("pool", xdst[i * 128:(i + 1) * 128, :], xn[:], [xnB], wr, xnB)
            if debug and tapname is not None:
                sc.dma("pool", dbg[tapname][i * 128:(i + 1) * 128, :], xn[:], [xnB], [], xnB)
            return xn, xnB

        def resid_tile2(i, xn, xnB, g_next, g_nextB, rots, pp):
            xt_rot, tmp_rot, xn_rot, hb_rot, jk_rot = rots
            if g_next is not None:
                return norm_a(xn[:], xnB, g_next, g_nextB, hb_rot, jk_rot)
            return None

        def resid_tile3(i, hbp, pp):
            if hbp is not None:
                norm_b(hbp[0], hbp[1], i, pp)

        def resid_rots(ph):
            return (Rot(nc, ph, "rxt", [128, D], F32, 3), Rot(nc, ph, "rtm", [128, D], F32, 2), Rot(nc, ph, "rxn", [128, D], F32, 3),
                    Rot(nc, ph, "rhb", [128, D], BF16, 3), Rot(nc, ph, "rjk", [128, D], BF16, 2))

        def phase_r1(l):
            with ExitStack() as ph:
                wo = sbt(ph, "wo", [128, KC, D], BF16); woB = [Buf(f"wo{j}") for j in range(4)]
                rots = resid_rots(ph)
                pp = PsPool(range(8))
                for j in range(4):
                    load_w(wo[:, :, j * 256:(j + 1) * 256], woB[j], wview(wo_d[l], 0, D, j * 256, 256), KC, 256)
                g_post, g_postB = load_gbc(l, 1)
                g_next, g_nextB = load_gbc(l, 2)
                xsrc = x_in if l == 0 else xs_d
                def s1(i):
                    halves = []
                    for hf in range(2):
                        ps, psb = pp.get()
                        for c in range(KC):
                            mm(ps[:], mg[:, c, i * 128:(i + 1) * 128], wo[:, c, hf * 512:(hf + 1) * 512], c == 0, c == KC - 1,
                               [mgB[i // 4], woB[2 * hf], woB[2 * hf + 1]], psb)
                        halves.append((ps, psb))
                    return halves
                rc = {}

                def r_a(i):
                    rc[i] = (s1(i), resid_load(i, xsrc, l > 0, rots))

                def r_b(i):
                    rc[i] = resid_tile(i, rc[i][0], rc[i][1], xs_d, True, g_post, g_postB, g_next, g_nextB, rots, pp, tapname=("x1" if l == 0 else None))

                def r_c(i):
                    xn, xnB = rc[i]
                    rc[i] = resid_tile2(i, xn, xnB, g_next, g_nextB, rots, pp)

                def r_d(i):
                    resid_tile3(i, rc.pop(i), pp)
                skew(NT, [r_a, r_b, r_c, r_d])
                sc.barrier()

        def phase_f(l, last):
            with ExitStack() as ph:
                actT = sbt(ph, "actT", [128, NJ, 1024], BF16); actB = [Buf(f"actT{q}") for q in range(2)]
                wout_lo = mg[:].rearrange("p c (a n) -> p (c a) n", a=2)
                wout_hi = sbt(ph, "wout_hi", [128, NJ - 16, D], BF16)
                woutB = [Buf(f"wout{jp}") for jp in range(NJ // 2)]
                g_post, g_postB = load_gbc(l, 3)
                g_next, g_nextB = (None, None)
                for th in range(2):
                    with ExitStack() as ph2:
                        wrot = Rot(nc, ph2, "fw", [128, KC, 256], BF16, 4)
                        sg_rot = Rot(nc, ph2, "fsg", [128, 512], F32, 2)
                        pp = PsPool(range(8))
                        def fload(jp):
                            wg, wgB = wrot.get()
                            load_w(wg[:], wgB, wview(wfi_d[l], 0, D, jp * 256, 256), KC, 256, cast_eng="dve")
                            wu, wuB = wrot.get()
                            load_w(wu[:], wuB, wview(wfi_d[l], 0, D, DFF + jp * 256, 256), KC, 256, cast_eng="dve")
                            return wg, wgB, wu, wuB
                        nxt = fload(0)
                        for jp in range(NJ // 2):
                            wg, wgB, wu, wuB = nxt
                            if jp + 1 < NJ // 2:
                                nxt = fload(jp + 1)
                            for cc in range(2):
                                j = 2 * jp + cc
                                for qq in range(2):
                                    Q = 2 * th + qq
                                    ps, psb = pp.get()
                                    proj_fm(ps, psb, wg[:, :, cc * 128:(cc + 1) * 128], wgB, Q)
                                    sg, sgB = sg_rot.get()
                                    sc.op("act", lambda e: e.activation(out=sg[:], in_=ps[:], func=AF.Silu), [psb], [sgB])
                                    ps2, ps2b = pp.get()
                                    proj_fm(ps2, ps2b, wu[:, :, cc * 128:(cc + 1) * 128], wuB, Q)
                                    sc.op("dve", lambda e: e.tensor_tensor(out=actT[:, j, qq * 512:(qq + 1) * 512], in0=ps2[:], in1=sg[:], op=ALU.mult),
                                          [ps2b, sgB], [actB[qq]])
                        sc.barrier()
                    with ExitStack() as ph2:
                        rots = resid_rots(ph2)
                        pp = PsPool(range(8))
                        if th == 0:
                            for jp in range(NJ // 2):
                                dst = wout_lo[:, 2 * jp:2 * jp + 2, :] if jp < 8 else wout_hi[:, 2 * jp - 16:2 * jp - 14, :]
                                load_w(dst, woutB[jp], wview(wfo_d[l], jp * 256, 256, 0, D), 2, D, cast_eng="dve")
                        if not last and g_next is None:
                            g_next, g_nextB = load_gbc(l + 1, 0)
                        def s1(ii):
                            halves = []
                            for hf in range(2):
                                ps, psb = pp.get()
                                for j in range(NJ):
                                    wj = wout_lo[:, j, hf * 512:(hf + 1) * 512] if j < 16 else wout_hi[:, j - 16, hf * 512:(hf + 1) * 512]
                                    mm(ps[:], actT[:, j, ii * 128:(ii + 1) * 128], wj, j == 0, j == NJ - 1, [actB[ii // 4], woutB[j // 2]], psb)
                                halves.append((ps, psb))
                            return halves
                        rc = {}

                        def r_a(ii):
                            rc[ii] = (s1(ii), resid_load(8 * th + ii, xs_d, True, rots))

                        def r_b(ii):
                            rc[ii] = resid_tile(8 * th + ii, rc[ii][0], rc[ii][1], out_d if last else xs_d, not last, g_post, g_postB, g_next, g_nextB, rots, pp,
                                                tapname=("x2" if l == 0 else None))

                        def r_c(ii):
                            xn, xnB = rc[ii]
                            rc[ii] = resid_tile2(8 * th + ii, xn, xnB, g_next, g_nextB, rots, pp)

                        def r_d(ii):
                            resid_tile3(8 * th + ii, rc.pop(ii), pp)
                        skew(8, [r_a, r_b, r_c, r_d])
                        sc.barrier()

        phase_n0()
        for l in range(nlayers):
            if "b" in only:
                phase_b(l, True)
            if "a" in only:
                phase_a(l, "b" not in only)
            if "c" in only:
                phase_c(l, "b" not in only and "a" not in only)
            if "r" in only:
                phase_r1(l)
            if "f" in only:
                phase_f(l, l == nlayers - 1)
        sc.finish()
    return nc


def prep_shared(inp):
    f = lambda a: np.ascontiguousarray(np.asarray(a, dtype=np.float32))
    sh = {}
    sh["w_in"] = f(inp["w_in"])
    sh["gvec"] = f(np.stack([inp["g_pre_mix"], inp["g_post_mix"], inp["g_pre_ffn"], inp["g_post_ffn"],
                             inp["gmlp_ln_g"], inp["gmlp_ln_b"]], axis=1))
    pv = np.zeros((DEPTH, 128, KC, 8), np.float32)
    for l in range(DEPTH):
        def col(v):
            return np.asarray(v, np.float32).reshape(KC, 128).T
        for j in range(4):
            pv[l, :, :, j] = col(inp["rnn_conv_w"][l, j])
        pv[l, :, :, 4] = col(inp["rnn_conv_b"][l])
        pv[l, :, :, 5] = col(inp["rnn_ba"][l])
        pv[l, :, :, 6] = col(inp["rnn_bx"][l])
        pv[l, :, :, 7] = col(inp["rnn_lam"][l])
    sh["pvec"] = f(pv.reshape(DEPTH, 128, KC * 8))
    sh["gmlp_ws"] = f(inp["gmlp_ws"])
    sh["gmlp_bs"] = f(np.asarray(inp["gmlp_bs"]).reshape(DEPTH, 1, 512))
    sh["peT"] = f(np.stack([np.transpose(inp["nsa_pe_k"], (0, 2, 1)), np.transpose(inp["nsa_pe_v"], (0, 2, 1))], axis=1))
    sh["nsa_w1"] = f(np.stack([inp["nsa_wk1"], inp["nsa_wv1"]], axis=1))
    sh["nsa_w2"] = f(np.stack([inp["nsa_wk2"], inp["nsa_wv2"]], axis=1))
    bd = np.zeros((DEPTH, 2, KC, 128, 128), np.float32)
    for a, nm in enumerate(("rnn_wa", "rnn_wx")):
        w = np.asarray(inp[nm], np.float32)
        for c in range(KC):
            bd[:, a, c, 0:64, 0:64] = w[:, 2 * c]
            bd[:, a, c, 64:128, 64:128] = w[:, 2 * c + 1]
    sh["rnn_bd"] = bd
    sh["w_br"] = f(np.stack([inp["w_br_a"], inp["w_br_b"], inp["w_br_c"]], axis=1))
    sh["w_o"] = f(inp["w_o"])
    sh["w_ffn_in"] = f(inp["w_ffn_in"])
    sh["w_ffn_out"] = f(inp["w_ffn_out"])
    p = np.arange(128)
    sq = np.zeros((4, 128, 128), np.float32)
    sq[0] = np.eye(128)
    sq[1] = (p[:, None] >= p[None, :])
    sq[2] = np.where(p[:, None] <= p[None, :], 0.0, -NEGM)
    sq[3] = np.where(p[:, None] > p[None, :], 0.0, -NEGM)
    sh["c_sq"] = sq
    t = np.arange(S)
    n = np.arange(128)
    valid = ((n[:, None] * 16 + 31) <= t[None, :]) & (n[:, None] < NCMP)
    sh["c_valid"] = np.where(valid, 0.0, -NEGM).astype(np.float32)
    sh["c_E"] = (np.arange(32)[:, None] == (t[None, :] // 64)).astype(np.float32)
    cur = (t // 64).reshape(NT, 128).T[:, :, None]
    j = np.arange(32)[None, None, :]
    forced = (j == 0) | (j == cur) | (j == cur - 1)
    future = j > cur
    keep = (~forced & ~future).astype(np.float32)
    add = np.where(forced, 1e4, np.where(future, -1e30, 0.0)).astype(np.float32)
    le = (~future).astype(np.float32)
    sh["c_msk"] = np.stack([keep, add, le]).reshape(3, 128, NT * 32).astype(np.float32)
    cs = np.arange(NCMP) * 16
    ss = np.arange(32) * 64
    ovl = np.clip(np.minimum(cs[:, None] + 32, ss[None, :] + 64) - np.maximum(cs[:, None], ss[None, :]), 0, None) / 32.0
    ov = np.zeros((128, 33), np.float32)
    ov[:NCMP, 0] = 1.0
    ov[:NCMP, 1:] = ovl
    sh["c_ov"] = ov
    return sh


def kernel(**inputs):
    sh = prep_shared(inputs)
    nc = build_program(debug=False)
    x = np.asarray(inputs["x"], np.float32)
    n = x.shape[0]
    in_maps = [dict(sh, x=np.ascontiguousarray(x[b])) for b in range(n)]
    res = run_bass_kernel_spmd(nc, in_maps, core_ids=list(range(n)))
    return np.stack([np.asarray(r["out"], dtype=np.float32) for r in res.results], axis=0)
```

```python
import numpy as np
from contextlib import ExitStack
import concourse.bass as bass
import concourse.mybir as mybir
from concourse.bass_utils import run_bass_kernel_spmd

F32 = mybir.dt.float32
BF16 = mybir.dt.bfloat16
AF = mybir.ActivationFunctionType
ALU = mybir.AluOpType
AX = mybir.AxisListType

S = 2048
D = 1024
NT = 16
NQ = 4
KC = 8
DEPTH = 2
DFF = 2816
NJ = 22
D_IN = 9776
OFF_U, OFF_V, OFF_Q, OFF_KV, OFF_NG, OFF_XR, OFF_RG, OFF_MG = 0, 1024, 2048, 3072, 4608, 4656, 5680, 6704
NCMP = 127
EPS = 1e-6
NEGM = 30000.0
BSTOP = 99


class Buf:
    __slots__ = ("name", "w", "r", "sem", "dcnt")

    def __init__(self, name):
        self.name = name
        self.w = None
        self.r = {}
        self.sem = None
        self.dcnt = 0


class Eng:
    def __init__(self, name, be, sem):
        self.name, self.be, self.sem = name, be, sem
        self.cnt = 0
        self.seen = {}


class Sched:
    def __init__(self, nc, stack):
        self.nc = nc
        self.stack = stack
        self.engs = {}
        for name, be in (("pe", nc.tensor), ("act", nc.scalar), ("dve", nc.vector),
                         ("pool", nc.gpsimd), ("sp", nc.sync)):
            sem = stack.enter_context(nc.semaphore("sem_" + name))
            self.engs[name] = Eng(name, be, sem)
        self.dma_bufs = []
        self.nwait = 0

    def _wait(self, E, ev):
        sem, val = ev
        k = id(sem)
        if E.seen.get(k, 0) >= val:
            return
        E.be.wait_ge(sem, val)
        E.seen[k] = val
        self.nwait += 1

    def _deps(self, E, reads, writes):
        for b in reads:
            if b.w is not None:
                if b.w[0] is E.sem and E.name == "pe":
                    continue
                self._wait(E, b.w)
        for b in writes:
            if b.w is not None and not (b.w[0] is E.sem and E.name == "pe"):
                self._wait(E, b.w)
            for ev in b.r.values():
                if not (ev[0] is E.sem and E.name == "pe"):
                    self._wait(E, ev)

    def _record(self, ev, reads, writes):
        k = id(ev[0])
        for b in reads:
            old = b.r.get(k)
            if old is None or old[1] < ev[1]:
                b.r[k] = ev
        for b in writes:
            b.w = ev
            b.r = {}

    def op(self, eng, fn, reads=(), writes=(), inc=True):
        E = self.engs[eng]
        self._deps(E, reads, writes)
        ins = fn(E.be)
        if inc:
            ins.then_inc(E.sem, 1)
            E.cnt += 1
            ev = (E.sem, E.cnt)
        else:
            ev = (E.sem, E.cnt + 1)
        self._record(ev, reads, writes)
        return ins

    def dma(self, q, out_ap, in_ap, reads, writes, sb, **kw):
        E = self.engs[q]
        self._deps(E, reads, writes)
        if sb.sem is None:
            sb.sem = self.stack.enter_context(self.nc.semaphore(f"dsem_{sb.name}_{_uid()}"))
            self.dma_bufs.append(sb)
        E.be.dma_start(out=out_ap, in_=in_ap, **kw).then_inc(sb.sem, 16)
        sb.dcnt += 1
        ev = (sb.sem, 16 * sb.dcnt)
        self._record(ev, reads, writes)

    def barrier(self):
        evs = [(E.sem, E.cnt) for E in self.engs.values() if E.cnt > 0]
        evs += [(b.sem, 16 * b.dcnt) for b in self.dma_bufs if b.dcnt > 0]
        for E in self.engs.values():
            for ev in evs:
                if ev[0] is E.sem:
                    continue
                self._wait(E, ev)

    def finish(self):
        E = self.engs["sp"]
        for b in self.dma_bufs:
            if b.dcnt > 0:
                self._wait(E, (b.sem, 16 * b.dcnt))
        for E2 in self.engs.values():
            if E2 is not E and E2.cnt > 0:
                self._wait(E, (E2.sem, E2.cnt))


_UID = [0]


def _uid():
    _UID[0] += 1
    return _UID[0]


def skew(n, stages, lag=1):
    ns = len(stages)
    for t in range(n + (ns - 1) * lag):
        for k in range(ns - 1, -1, -1):
            i = t - k * lag
            if 0 <= i < n:
                stages[k](i)


class Rot:
    def __init__(self, nc, stack, name, shape, dtype, n):
        self.t = []
        for i in range(n):
            t = stack.enter_context(nc.sbuf_tensor(f"sb_{name}{i}_{_uid()}", shape, dtype))
            self.t.append((t, Buf(f"{name}{i}")))
        self.i = 0

    def get(self):
        r = self.t[self.i % len(self.t)]
        self.i += 1
        return r


def build_program(debug=False, nlayers=DEPTH, only="bacrf"):
    nc = bass.Bass("TRN2", target_bir_lowering=False)

    def dram(name, shape, kind="ExternalInput", dtype=F32):
        return nc.dram_tensor(name, list(shape), dtype, kind=kind).ap()

    x_in = dram("x", [S, D])
    out_d = dram("out", [S, D], kind="ExternalOutput")
    xs_d = dram("xs", [S, D], kind="Internal")
    w_in = dram("w_in", [DEPTH, D, D_IN])
    gvec = dram("gvec", [DEPTH, 6, D])
    pvec_d = dram("pvec", [DEPTH, 128, KC * 8])
    gws_d = dram("gmlp_ws", [DEPTH, 4, 128, 128])
    gbs_d = dram("gmlp_bs", [DEPTH, 1, 512])
    peT_d = dram("peT", [DEPTH, 2, 64, 32])
    w1_d = dram("nsa_w1", [DEPTH, 2, 2048, 256])
    w2_d = dram("nsa_w2", [DEPTH, 2, 256, 64])
    rbd_d = dram("rnn_bd", [DEPTH, 2, KC, 128, 128])
    wbr_d = dram("w_br", [DEPTH, 3, D, D])
    wo_d = dram("w_o", [DEPTH, D, D])
    wfi_d = dram("w_ffn_in", [DEPTH, D, 2 * DFF])
    wfo_d = dram("w_ffn_out", [DEPTH, DFF, D])
    c_sq = dram("c_sq", [4, 128, 128])
    c_valid = dram("c_valid", [128, S])
    c_E = dram("c_E", [32, S])
    c_msk = dram("c_msk", [3, 128, NT * 32])
    c_ov = dram("c_ov", [128, 33])
    dbg = {}
    if debug:
        for nm in ("ybT", "yaT", "ycT"):
            dbg[nm] = dram("dbg_" + nm, [128, KC * S], kind="ExternalOutput", dtype=BF16)
        dbg["x1"] = dram("dbg_x1", [S, D], kind="ExternalOutput")
        dbg["x2"] = dram("dbg_x2", [S, D], kind="ExternalOutput")

    with ExitStack() as top:
        sc = Sched(nc, top)

        def sbt(stack, name, shape, dtype):
            return stack.enter_context(nc.sbuf_tensor(f"sb_{name}_{_uid()}", list(shape), dtype))

        psum = [top.enter_context(nc.psum_tensor(f"ps{i}", [128, 512], F32)) for i in range(8)]
        psB = [Buf(f"ps{i}") for i in range(8)]

        class PsPool:
            def __init__(self, banks):
                self.banks, self.i = list(banks), 0

            def get(self):
                b = self.banks[self.i % len(self.banks)]
                self.i += 1
                return psum[b], psB[b]

        hT = sbt(top, "hT", [128, KC, S], BF16); hTB = [Buf(f"hT{q}") for q in range(NT)]
        mg = sbt(top, "merged", [128, KC, S], BF16); mgB = [Buf(f"mg{q}") for q in range(NQ)]
        identb = sbt(top, "identb", [128, 128], BF16)
        tril = sbt(top, "tril", [128, 128], F32)
        negcaus = sbt(top, "negcaus", [128, 128], BF16)
        neganti = sbt(top, "neganti", [128, 128], BF16)
        validb = sbt(top, "validb", [128, S], BF16)
        keepm = sbt(top, "keepm", [128, NT * 32], F32)
        addm = sbt(top, "addm", [128, NT * 32], F32)
        lem = sbt(top, "lem", [128, NT * 32], F32)
        ovc = sbt(top, "ovc", [128, 33], BF16)
        onesr = sbt(top, "onesr", [1, 128], BF16)
        kaug = sbt(top, "kaug", [96, S], BF16); kaugB = Buf("kaug")
        pvec = sbt(top, "pvec", [128, DEPTH, KC * 8], F32)
        constB = Buf("const")
        stg = Rot(nc, top, "stg", [128, 2048], F32, 2)
        gbc = Rot(nc, top, "gbc", [128, D], F32, 2)
        small = Rot(nc, top, "small", [128, 8], F32, 12)

        with ExitStack() as ph:
            t_sq = sbt(ph, "t_sq", [128, 3, 128], F32); t_sqB = Buf("t_sq")
            t_val = sbt(ph, "t_val", [128, S], F32); t_valB = Buf("t_val")
            t_E = sbt(ph, "t_E", [96, S], F32); t_EB = Buf("t_E")
            t_ov = sbt(ph, "t_ov", [128, 33], F32); t_ovB = Buf("t_ov")
            for j, src in enumerate((0, 2, 3)):
                sc.dma("sp", t_sq[:, j, :], c_sq[src], [], [t_sqB], t_sqB)
            sc.dma("sp", tril[:], c_sq[1], [], [constB], constB)
            sc.dma("sp", t_val[:], c_valid[:, :], [], [t_valB], t_valB)
            sc.dma("sp", t_E[64:96, :], c_E[:, :], [], [t_EB], t_EB)
            sc.dma("sp", t_ov[:], c_ov[:, :], [], [t_ovB], t_ovB)
            sc.dma("sp", keepm[:], c_msk[0], [], [constB], constB)
            sc.dma("sp", addm[:], c_msk[1], [], [constB], constB)
            sc.dma("sp", lem[:], c_msk[2], [], [constB], constB)
            for l in range(DEPTH):
                sc.dma("sp", pvec[:, l, :], pvec_d[l], [], [constB], constB)
            sc.op("dve", lambda e: e.tensor_copy(out=identb[:], in_=t_sq[:, 0, :]), [t_sqB], [constB])
            sc.op("dve", lambda e: e.tensor_copy(out=negcaus[:], in_=t_sq[:, 1, :]), [t_sqB], [constB])
            sc.op("dve", lambda e: e.tensor_copy(out=neganti[:], in_=t_sq[:, 2, :]), [t_sqB], [constB])
            sc.op("dve", lambda e: e.tensor_copy(out=validb[:], in_=t_val[:]), [t_valB], [constB])
            sc.op("act", lambda e: e.copy(out=kaug[64:96, :], in_=t_E[64:96, :]), [t_EB], [kaugB])
            sc.op("dve", lambda e: e.tensor_copy(out=ovc[:], in_=t_ov[:]), [t_ovB], [constB])
            sc.op("dve", lambda e: e.memset(onesr[:], 1.0), [], [constB])
            sc.barrier()

        def load_w(dst, dstB, src, kk, nn, cast_eng="pool", parts=128):
            assert kk * nn <= 2048
            st, stB = stg.get()
            sv = st[0:parts, 0:kk * nn].rearrange("p (k n) -> p k n", k=kk)
            sc.dma("sp", sv, src, [], [stB], stB)
            sc.op(cast_eng, lambda e: e.tensor_copy(out=dst, in_=sv), [stB], [dstB])

        def wview(w2d, r0, nrows, c0, ncols):
            return w2d[r0:r0 + nrows, c0:c0 + ncols].rearrange("(k p) n -> p k n", p=128)

        def load_gbc(l, idx):
            t, tB = gbc.get()
            sc.dma("sp", t[:], gvec[l, idx:idx + 1, :].partition_broadcast(128), [], [tB], tB)
            return t, tB

        def mm(out, lhsT, rhs, start, stop, reads, wB):
            sc.op("pe", lambda e: e.matmul(out, lhsT=lhsT, rhs=rhs, start=start, stop=stop),
                  reads, [wB], inc=stop)

        def proj_fm(ps, psb, w_ap, wB, Q, M=128, src=None, srcB=None):
            src = hT if src is None else src
            for k in range(KC):
                rb = [wB] + (hTB[4 * Q:4 * Q + 4] if srcB is None else srcB)
                mm(ps[0:M, :], w_ap[:, k, :], src[:, k, Q * 512:(Q + 1) * 512], k == 0, k == KC - 1, rb, psb)

        def rstd_from_ss(ss_ap, ssB, n):
            sc.op("act", lambda e: e.activation(out=ss_ap, in_=ss_ap, func=AF.Sqrt, scale=1.0 / n, bias=EPS), [ssB], [ssB])
            sc.op("dve", lambda e: e.reciprocal(out=ss_ap, in_=ss_ap), [ssB], [ssB])

        def norm_a(xt, xtB, g_t, gB, hb_rot, junk_rot):
            sm, smB = small.get()
            jk, jkB = junk_rot.get()
            sc.op("act", lambda e: e.activation(out=jk[:], in_=xt, func=AF.Square, accum_out=sm[:, 0:1]), [xtB], [jkB, smB])
            rstd_from_ss(sm[:, 0:1], smB, D)
            hb, hbB = hb_rot.get()
            sc.op("dve", lambda e: e.scalar_tensor_tensor(out=hb[:], in0=xt, scalar=sm[:, 0:1], in1=g_t[:], op0=ALU.mult, op1=ALU.mult),
                  [xtB, smB, gB], [hbB])
            return hb, hbB

        def norm_b(hb, hbB, i, pspool):
            ps, psb = pspool.get()
            pv = ps[:].bitcast(BF16)
            for c in range(KC):
                sc.op("pe", lambda e: e.transpose(out=pv[:, c * 128:(c + 1) * 128], in_=hb[:, c * 128:(c + 1) * 128], identity=identb[:]),
                      [hbB, constB], [psb], inc=(c == KC - 1))
            sc.op("act", lambda e: e.copy(out=hT[:, :, i * 128:(i + 1) * 128], in_=pv.rearrange("p (c t) -> p c t", c=KC)), [psb], [hTB[i]])

        def norm_tile_to_hT(xt, xtB, g_t, gB, i, pspool, hb_rot, junk_rot):
            hb, hbB = norm_a(xt, xtB, g_t, gB, hb_rot, junk_rot)
            norm_b(hb, hbB, i, pspool)

        def tap(name, src_ap, srcB, shape3=True):
            if debug and name in dbg:
                d = dbg[name]
                if shape3:
                    d = d.rearrange("p (c t) -> p c t", c=KC)
                sc.dma("sp", d, src_ap, [srcB], [], srcB)

        xsB = [Buf(f"xs{i}") for i in range(NT)]

        def phase_n0():
            with ExitStack() as ph:
                xt_rot = Rot(nc, ph, "n0x", [128, D], F32, 6)
                hb_rot = Rot(nc, ph, "n0h", [128, D], BF16, 3)
                jk_rot = Rot(nc, ph, "n0j", [128, D], BF16, 2)
                pp = PsPool(range(8))
                g_t, gB = load_gbc(0, 0)
                nctx = {}

                def n_a(i):
                    xt, xtB = xt_rot.get()
                    sc.dma("sp", xt[:], x_in[i * 128:(i + 1) * 128, :], [], [xtB], xtB)
                    nctx[i] = (xt, xtB)

                def n_a2(i):
                    pass

                def n_b(i):
                    xt, xtB = nctx[i]
                    nctx[i] = norm_a(xt[:], xtB, g_t, gB, hb_rot, jk_rot)

                def n_c(i):
                    hb, hbB = nctx.pop(i)
                    norm_b(hb, hbB, i, pp)
                skew(NT, [n_a, n_a2, n_b, n_c])
                sc.barrier()

        def merge_branch(l, k, yT, yTB, first):
            with ExitStack() as ph:
                wrot = Rot(nc, ph, "mw", [128, KC, 256], BF16, 4)
                sg_rot = Rot(nc, ph, "msg", [128, 512], F32, 2)
                tmp_rot = Rot(nc, ph, "mtmp", [128, 512], BF16, 2)
                pp = PsPool(range(8))
                def mload(cp):
                    wg, wgB = wrot.get()
                    load_w(wg[:], wgB, wview(w_in[l], 0, D, OFF_MG + k * D + cp * 256, 256), KC, 256, cast_eng="dve")
                    wb, wbB = wrot.get()
                    load_w(wb[:], wbB, wview(wbr_d[l, k], 0, D, cp * 256, 256), KC, 256, cast_eng="dve")
                    return wg, wgB, wb, wbB
                nxt = mload(0)
                for cp in range(4):
                    wg, wgB, wb, wbB = nxt
                    if cp + 1 < 4:
                        nxt = mload(cp + 1)
                    for cc in range(2):
                        co = cp * 2 + cc
                        for Q in range(NQ):
                            ps, psb = pp.get()
                            proj_fm(ps, psb, wg[:, :, cc * 128:(cc + 1) * 128], wgB, Q)
                            sg, sgB = sg_rot.get()
                            sc.op("act", lambda e: e.activation(out=sg[:], in_=ps[:], func=AF.Sigmoid), [psb], [sgB])
                            ps2, ps2b = pp.get()
                            proj_fm(ps2, ps2b, wb[:, :, cc * 128:(cc + 1) * 128], wbB, Q, src=yT, srcB=yTB(Q))
                            dst = mg[:, co, Q * 512:(Q + 1) * 512]
                            if first:
                                sc.op("dve", lambda e: e.tensor_tensor(out=dst, in0=ps2[:], in1=sg[:], op=ALU.mult), [ps2b, sgB], [mgB[Q]])
                            else:
                                tm, tmB = tmp_rot.get()
                                sc.op("dve", lambda e: e.tensor_tensor(out=tm[:], in0=ps2[:], in1=sg[:], op=ALU.mult), [ps2b, sgB], [tmB])
                                sc.op("pool", lambda e: e.tensor_tensor(out=dst, in0=dst, in1=tm[:], op=ALU.add), [tmB, mgB[Q]], [mgB[Q]])
                sc.barrier()

        def phase_c(l, first):
            L = S // 2
            with ExitStack() as ph:
                ycT = sbt(ph, "ycT", [128, KC, S], BF16); ycB = [Buf(f"ycT{q}") for q in range(NQ)]
                ph_out = ph
                ph = ExitStack()
                wrot = Rot(nc, ph, "cw", [128, KC, 256], BF16, 4)
                bdrot = Rot(nc, ph, "cbd", [128, 1, 128], BF16, 4)
                sets = []
                for si in range(2):
                    d = {}
                    for nm, shp, dt_ in (("xr", [128, L + 3], F32), ("xc", [128, L], F32), ("r", [128, L], F32), ("i", [128, L], F32),
                                         ("xb", [128, L], BF16), ("g", [128, L], BF16), ("cxr", [128, 4], F32), ("ch", [128, 2], F32)):
                        d[nm] = (sbt(ph, f"c{nm}{si}", shp, dt_), Buf(f"c{nm}{si}"))
                    sets.append(d)
                coef = sbt(ph, "coef", [128, KC], F32); coefB = Buf("coef")
                coef2 = sbt(ph, "coef2", [128, KC], F32)
                pp = PsPool(range(8))
                pv = pvec[:, l, :].rearrange("p (c k) -> p c k", k=8)
                sc.op("act", lambda e: e.activation(out=coef[:], in_=pv[:, :, 7], func=AF.Exp, scale=-1.0), [constB], [coefB])
                sc.op("act", lambda e: e.activation(out=coef[:], in_=coef[:], func=AF.Ln, bias=1.0), [coefB], [coefB])
                sc.op("dve", lambda e: e.tensor_scalar(out=coef[:], in0=coef[:], scalar1=-8.0, scalar2=None, op0=ALU.mult), [coefB], [coefB])
                sc.op("dve", lambda e: e.tensor_scalar(out=coef2[:], in0=coef[:], scalar1=2.0, scalar2=None, op0=ALU.mult), [coefB], [coefB])

                def unit(c, hf, st, wx, wxB, wr, wrB, bd):
                    cc = c % 2
                    (xr, xrB), (xc, xcB), (tr, rB), (ti, iB) = st["xr"], st["xc"], st["r"], st["i"]
                    (xb, xbB), (tg, gB_), (cxr, cxrB), (ch, chB) = st["xb"], st["g"], st["cxr"], st["ch"]
                    bda, bdaB, bdx, bdxB = bd
                    if hf == 0:
                        sc.op("dve", lambda e: e.memset(xr[:, 0:3], 0.0), [], [xrB])
                    else:
                        sc.op("dve", lambda e: e.tensor_copy(out=xr[:, 0:3], in_=cxr[:, 0:3]), [cxrB], [xrB])
                    for qq in range(2):
                        Q = 2 * hf + qq
                        ps, psb = pp.get()
                        proj_fm(ps, psb, wx[:, :, cc * 128:(cc + 1) * 128], wxB, Q)
                        sc.op("act", lambda e: e.copy(out=xr[:, 3 + qq * 512:3 + (qq + 1) * 512], in_=ps[:]), [psb], [xrB])
                    yield
                    for qq in range(2):
                        Q = 2 * hf + qq
                        ps, psb = pp.get()
                        proj_fm(ps, psb, wr[:, :, cc * 128:(cc + 1) * 128], wrB, Q)
                        sc.op("act", lambda e: e.activation(out=tg[:, qq * 512:(qq + 1) * 512], in_=ps[:], func=AF.Gelu_apprx_tanh), [psb], [gB_])
                    yield
                    sc.op("dve", lambda e: e.tensor_scalar(out=xc[:], in0=xr[:, 3:3 + L], scalar1=pv[:, c, 3:4], scalar2=pv[:, c, 4:5],
                                                           op0=ALU.mult, op1=ALU.add), [xrB, constB], [xcB])
                    yield
                    for j in range(3):
                        sc.op("dve", lambda e: e.scalar_tensor_tensor(out=xc[:], in0=xr[:, j:j + L], scalar=pv[:, c, j:j + 1], in1=xc[:],
                                                                      op0=ALU.mult, op1=ALU.add), [xrB, xcB, constB], [xcB])
                        yield
                    if hf == 0:
                        sc.op("dve", lambda e: e.tensor_copy(out=cxr[:, 0:3], in_=xr[:, L:L + 3]), [xrB], [cxrB])
                    sc.op("act", lambda e: e.copy(out=xb[:], in_=xc[:]), [xcB], [xbB])
                    yield
                    for qq in range(2):
                        ps, psb = pp.get()
                        mm(ps[:], bda[:, 0, :], xb[:, qq * 512:(qq + 1) * 512], True, True, [bdaB, xbB], psb)
                        sc.op("act", lambda e: e.activation(out=tr[:, qq * 512:(qq + 1) * 512], in_=ps[:], func=AF.Sigmoid, bias=pv[:, c, 5:6]),
                              [psb, constB], [rB])
                        ps, psb = pp.get()
                        mm(ps[:], bdx[:, 0, :], xb[:, qq * 512:(qq + 1) * 512], True, True, [bdxB, xbB], psb)
                        sc.op("act", lambda e: e.activation(out=ti[:, qq * 512:(qq + 1) * 512], in_=ps[:], func=AF.Sigmoid, bias=pv[:, c, 6:7]),
                              [psb, constB], [iB])
                    yield
                    sc.op("dve", lambda e: e.tensor_tensor(out=ti[:], in0=ti[:], in1=xc[:], op=ALU.mult), [iB, xcB], [iB])
                    yield
                    sc.op("act", lambda e: e.activation(out=xc[:], in_=tr[:], func=AF.Exp, scale=coef2[:, c:c + 1]), [rB, coefB, iB], [xcB])
                    sc.op("act", lambda e: e.activation(out=tr[:], in_=tr[:], func=AF.Exp, scale=coef[:, c:c + 1]), [rB, coefB], [rB])
                    yield
                    sc.op("act", lambda e: e.activation(out=xc[:], in_=xc[:], func=AF.Sqrt, scale=-1.0, bias=1.0), [xcB], [xcB])
                    yield
                    sc.op("dve", lambda e: e.tensor_tensor(out=ti[:], in0=ti[:], in1=xc[:], op=ALU.mult), [iB, xcB], [iB])
                    yield
                    if hf == 0:
                        sc.op("dve", lambda e: e.tensor_tensor_scan(out=xc[:], data0=tr[:], data1=ti[:], initial=0.0, op0=ALU.mult, op1=ALU.add),
                              [rB, iB], [xcB])
                        sc.op("dve", lambda e: e.tensor_copy(out=ch[:, 0:1], in_=xc[:, L - 1:L]), [xcB], [chB])
                    else:
                        sc.op("dve", lambda e: e.tensor_tensor_scan(out=xc[:], data0=tr[:], data1=ti[:], initial=ch[:, 0:1], op0=ALU.mult, op1=ALU.add),
                              [rB, iB, chB], [xcB])
                    yield
                    sc.op("dve", lambda e: e.tensor_tensor(out=ycT[:, c, hf * L:(hf + 1) * L], in0=xc[:], in1=tg[:], op=ALU.mult), [xcB, gB_], ycB[2 * hf:2 * hf + 2])

                for cp in range(KC // 2):
                    wx, wxB = wrot.get()
                    load_w(wx[:], wxB, wview(w_in[l], 0, D, OFF_XR + cp * 256, 256), KC, 256)
                    wr, wrB = wrot.get()
                    load_w(wr[:], wrB, wview(w_in[l], 0, D, OFF_RG + cp * 256, 256), KC, 256)
                    bds = []
                    for cc in range(2):
                        c = 2 * cp + cc
                        bda, bdaB = bdrot.get()
                        load_w(bda[:], bdaB, rbd_d[l, 0, c].rearrange("p (k n) -> p k n", k=1), 1, 128)
                        bdx, bdxB = bdrot.get()
                        load_w(bdx[:], bdxB, rbd_d[l, 1, c].rearrange("p (k n) -> p k n", k=1), 1, 128)
                        bds.append((bda, bdaB, bdx, bdxB))
                    for hf in range(2):
                        gens = [unit(2 * cp + cc, hf, sets[cc], wx, wxB, wr, wrB, bds[cc]) for cc in range(2)]
                        alive = True
                        while alive:
                            alive = False
                            for gn in gens:
                                try:
                                    next(gn)
                                    alive = True
                                except StopIteration:
                                    pass
                sc.barrier()
                ph.close()
                tap("ycT", ycT[:], ycB[0])
                merge_branch(l, 2, ycT, lambda Q: [ycB[Q]], first)

        def phase_a(l, first):
            with ExitStack() as ph:
                yaT = sbt(ph, "yaT", [128, KC, S], BF16); yaB = [Buf(f"yaT{q}") for q in range(NQ)]
                ph2 = ExitStack()
                wv = sbt(ph2, "wv", [128, KC, D], BF16); wvB = [Buf(f"wv{j}") for j in range(4)]
                wsT = sbt(ph2, "wsT", [128, 4, 128], BF16); wsTB = Buf("wsT")
                wsf = sbt(ph2, "wsf", [128, 4, 128], F32); wsfB = Buf("wsf")
                wsm = sbt(ph2, "wsm", [128, 4, 128], BF16); wsmB = Buf("wsm")
                bsf = sbt(ph2, "bsf", [1, 512], F32); bsfB = Buf("bsf")
                bsr = sbt(ph2, "bsr", [1, 512], BF16); bsrB = Buf("bsr")
                wrot = Rot(nc, ph2, "aw", [128, KC, 256], BF16, 2)
                vg_rot = Rot(nc, ph2, "avg", [128, D], F32, 4)
                vn_rot = Rot(nc, ph2, "avn", [128, D], BF16, 3)
                st_rot = Rot(nc, ph2, "ast", [128, 16], F32, 5)
                pp = PsPool(range(8))
                sc.dma("sp", wsf[:], gws_d[l].rearrange("g t s -> t g s"), [], [wsfB], wsfB)
                sc.dma("sp", bsf[:], gbs_d[l], [], [bsfB], bsfB)
                for g in range(4):
                    sc.op("dve", lambda e: e.tensor_tensor(out=wsm[:, g, :], in0=wsf[:, g, :], in1=tril[:], op=ALU.mult), [wsfB, constB], [wsmB])
                ps, psb = pp.get()
                pvw = ps[:].bitcast(BF16)
                for g in range(4):
                    sc.op("pe", lambda e: e.transpose(out=pvw[:, g * 128:(g + 1) * 128], in_=wsm[:, g, :], identity=identb[:]),
                          [wsmB, constB], [psb], inc=(g == 3))
                sc.op("act", lambda e: e.copy(out=wsT[:], in_=pvw[:, 0:512].rearrange("p (g t) -> p g t", g=4)), [psb], [wsTB])
                sc.op("dve", lambda e: e.tensor_copy(out=bsr[:], in_=bsf[:]), [bsfB], [bsrB])
                def aload(cp):
                    wu, wuB = wrot.get()
                    load_w(wu[:], wuB, wview(w_in[l], 0, D, OFF_U + cp * 256, 256), KC, 256)
                    return wu, wuB
                nxt = aload(0)
                for cp in range(4):
                    wu, wuB = nxt
                    if cp + 1 < 4:
                        nxt = aload(cp + 1)
                    for cc in range(2):
                        c = cp * 2 + cc
                        for Q in range(NQ):
                            ps, psb = pp.get()
                            proj_fm(ps, psb, wu[:, :, cc * 128:(cc + 1) * 128], wuB, Q)
                            sc.op("act", lambda e: e.activation(out=yaT[:, c, Q * 512:(Q + 1) * 512], in_=ps[:], func=AF.Gelu_apprx_tanh), [psb], [yaB[Q]])
                for j in range(4):
                    load_w(wv[:, :, j * 256:(j + 1) * 256], wvB[j], wview(w_in[l], 0, D, OFF_V + j * 256, 256), KC, 256)
                lng, lngB = load_gbc(l, 4)
                lnb, lnbB = load_gbc(l, 5)
                ctx = {}

                def a_s1(i):
                    vg, vgB = vg_rot.get()
                    for hf in range(2):
                        ps, psb = pp.get()
                        for k in range(KC):
                            mm(ps[:], hT[:, k, i * 128:(i + 1) * 128], wv[:, k, hf * 512:(hf + 1) * 512], k == 0, k == KC - 1,
                               [hTB[i], wvB[2 * hf], wvB[2 * hf + 1]], psb)
                        sc.op("act", lambda e: e.activation(out=vg[:, hf * 512:(hf + 1) * 512], in_=ps[:], func=AF.Gelu_apprx_tanh), [psb], [vgB])
                    ctx[i] = dict(vg=(vg, vgB))

                def a_s2(i):
                    vg, vgB = ctx[i]["vg"]
                    stt, sttB = st_rot.get()
                    for hf in range(2):
                        sc.op("dve", lambda e: e.bn_stats(out=stt[:, hf * 6:(hf + 1) * 6], in_=vg[:, hf * 512:(hf + 1) * 512]), [vgB], [sttB])
                    sc.op("dve", lambda e: e.bn_aggr(out=stt[:, 12:14], in_=stt[:, 0:12]), [sttB], [sttB])
                    sc.op("act", lambda e: e.activation(out=stt[:, 13:14], in_=stt[:, 13:14], func=AF.Sqrt, bias=EPS), [sttB], [sttB])
                    sc.op("dve", lambda e: e.reciprocal(out=stt[:, 13:14], in_=stt[:, 13:14]), [sttB], [sttB])
                    sc.op("dve", lambda e: e.scalar_tensor_tensor(out=vg[:], in0=vg[:], scalar=stt[:, 12:13], in1=lng[:], op0=ALU.subtract, op1=ALU.mult),
                          [vgB, sttB, lngB], [vgB])
                    ctx[i]["stt"] = (stt, sttB)

                def a_s3(i):
                    vg, vgB = ctx[i]["vg"]
                    stt, sttB = ctx[i]["stt"]
                    vn, vnB = vn_rot.get()
                    sc.op("dve", lambda e: e.scalar_tensor_tensor(out=vn[:], in0=vg[:], scalar=stt[:, 13:14], in1=lnb[:], op0=ALU.mult, op1=ALU.add),
                          [vgB, sttB, lnbB], [vnB])
                    ctx[i]["vn"] = (vn, vnB)

                def a_s4(i):
                    vn, vnB = ctx[i]["vn"]
                    for hf in range(2):
                        ps, psb = pp.get()
                        for cc in range(4):
                            c = hf * 4 + cc
                            g = c // 2
                            mm(ps[:, cc * 128:(cc + 1) * 128], vn[:, c * 128:(c + 1) * 128], wsT[:, g, :], True, False, [vnB, wsTB], psb)
                            mm(ps[:, cc * 128:(cc + 1) * 128], onesr[0:1, :], bsr[0:1, g * 128:(g + 1) * 128], False, True, [constB, bsrB], psb)
                        dst = yaT[:, hf * 4:hf * 4 + 4, i * 128:(i + 1) * 128]
                        sc.op("dve", lambda e: e.tensor_tensor(out=dst, in0=dst, in1=ps[:].rearrange("p (c t) -> p c t", c=4), op=ALU.mult),
                              [psb, yaB[i // 4]], [yaB[i // 4]])
                    del ctx[i]
                skew(NT, [a_s1, a_s2, a_s3, a_s4])
                sc.barrier()
                ph2.close()
                tap("yaT", yaT[:], yaB[0])
                merge_branch(l, 0, yaT, lambda Q: [yaB[Q]], first)

        def phase_b(l, first):
            with ExitStack() as ph:
                ybT = sbt(ph, "ybT", [128, KC, S], BF16); ybTB = [Buf(f"ybT{q}") for q in range(NQ)]
                ph2 = ExitStack()
                vslc = sbt(ph2, "vslc", [128, NT, 4, 65], BF16); vslcB = Buf("vslc")
                vwin = sbt(ph2, "vwin", [128, NT, 4, 65], BF16); vwinB = Buf("vwin")
                gates = sbt(ph2, "gates", [128, NT, 48], F32); gatesB = Buf("gates")
                kcT = sbt(ph2, "kcT", [96, 4, 128], BF16); kcTB = Buf("kcT")
                vca = sbt(ph2, "vca", [128, 4, 97], BF16); vcaB = Buf("vca")
                wrot = Rot(nc, ph2, "bw", [128, KC, 256], BF16, 2)
                pp = PsPool(range(2, 8))
                acc = PsPool([0, 1])
                sc.op("dve", lambda e: e.memset(vslc[:, :, :, 64:65], 1.0), [], [vslcB])
                sc.op("dve", lambda e: e.memset(vwin[:, :, :, 64:65], 1.0), [], [vwinB])
                sc.op("dve", lambda e: e.memset(vca[:], 0.0), [], [vcaB])
                sc.op("dve", lambda e: e.memset(kcT[:], 0.0), [], [kcTB])
                for g in range(4):
                    sc.op("dve", lambda e: e.tensor_copy(out=vca[:, g, 64:97], in_=ovc[:]), [constB], [vcaB])
                if BSTOP >= 1:
                    wng, wngB = wrot.get()
                    load_w(wng[:, :, 0:48], wngB, wview(w_in[l], 0, D, OFF_NG, 48), KC, 48)
                    for q in range(NQ):
                        ps, psb = pp.get()
                        for j in range(4):
                            i = 4 * q + j
                            for k in range(KC):
                                mm(ps[:, j * 48:(j + 1) * 48], hT[:, k, i * 128:(i + 1) * 128], wng[:, k, 0:48], k == 0, k == KC - 1, [hTB[i], wngB], psb)
                        sc.op("act", lambda e: e.activation(out=gates[:, 4 * q:4 * q + 4, :], in_=ps[:, 0:192].rearrange("p (j c) -> p j c", j=4), func=AF.Sigmoid),
                              [psb], [gatesB])
                if BSTOP >= 2:
                    for vt, vtB, blk in ((vslc, vslcB, 3), (vwin, vwinB, 5)):
                        wvv, wvvB = wrot.get()
                        load_w(wvv[:], wvvB, wview(w_in[l], 0, D, OFF_KV + blk * 256, 256), KC, 256)
                        for i in range(NT):
                            ps, psb = pp.get()
                            for k in range(KC):
                                mm(ps[:, 0:256], hT[:, k, i * 128:(i + 1) * 128], wvv[:, k, :], k == 0, k == KC - 1, [hTB[i], wvvB], psb)
                            eng = "act" if i % 2 == 0 else "dve"
                            if eng == "act":
                                sc.op("act", lambda e: e.copy(out=vt[:, i, :, 0:64], in_=ps[:, 0:256].rearrange("p (g d) -> p g d", g=4)), [psb], [vtB])
                            else:
                                sc.op("dve", lambda e: e.tensor_copy(out=vt[:, i, :, 0:64], in_=ps[:, 0:256].rearrange("p (g d) -> p g d", g=4)), [psb], [vtB])
                if BSTOP >= 3:
                    with ExitStack() as ph3:
                        cmpT = sbt(ph3, "cmpT", [64, 4, 16, 128], BF16); cmpTB = Buf("cmpT")
                        w1rot = Rot(nc, ph3, "w1t", [64, 32, 128], BF16, 2)
                        w2t = sbt(ph3, "w2t", [128, 2, 2, 64], BF16); w2B = Buf("w2t")
                        peT = sbt(ph3, "peT", [64, 2, 32], BF16); peTB = Buf("peT")
                        hid = sbt(ph3, "hid", [128, 4, 2, 128], BF16); hidB = Buf("hid")
                        cvec = sbt(ph3, "cvec", [128, 4], F32); cvecB = Buf("cvec")
                        for kv in range(2):
                            load_w(w2t[:, kv, :, :], w2B, w2_d[l, kv].rearrange("(jc p) d -> p jc d", p=128), 2, 64)
                            load_w(peT[:, kv:kv + 1, :], peTB, peT_d[l, kv].rearrange("d (a n) -> d a n", a=1), 1, 32, parts=64)
                        for kv in range(2):
                            wkc, wkcB = wrot.get()
                            load_w(wkc[:], wkcB, wview(w_in[l], 0, D, OFF_KV + kv * 256, 256), KC, 256)
                            for gp in range(2):
                                for Q in range(NQ):
                                    ps, psb = pp.get()
                                    proj_fm(ps, psb, wkc[:, :, gp * 128:(gp + 1) * 128], wkcB, Q, M=128)
                                    for gb in range(2):
                                        g = 2 * gp + gb
                                        sc.op("act", lambda e: e.copy(out=cmpT[:, g, :, Q * 32:(Q + 1) * 32].rearrange("d r n -> d n r"),
                                                                      in_=ps[gb * 64:(gb + 1) * 64, :].rearrange("d (n r) -> d n r", r=16)), [psb], [cmpTB])
                            w1v = w1_d[l, kv].rearrange("(l d) j -> d l j", d=64)
                            for jc in range(2):
                                w1t, w1B = w1rot.get()
                                for lh in range(2):
                                    load_w(w1t[:, lh * 16:(lh + 1) * 16, :], w1B, w1v[:, lh * 16:(lh + 1) * 16, jc * 128:(jc + 1) * 128], 16, 128, parts=64)
                                ps, psb = pp.get()
                                for li in range(32):
                                    mm(ps[:, 0:1], w1t[:, li, :], peT[:, kv, li:li + 1], li == 0, li == 31, [w1B, peTB], psb)
                                cv = cvec[:, kv * 2 + jc:kv * 2 + jc + 1]
                                sc.op("dve", lambda e: e.tensor_copy(out=cv, in_=ps[:, 0:1]), [psb], [cvecB])
                                for g in range(4):
                                    ps, psb = pp.get()
                                    for li in range(32):
                                        mm(ps[:, 0:NCMP], w1t[:, li, :], cmpT[:, g, li % 16, li // 16:li // 16 + NCMP], li == 0, li == 31, [w1B, cmpTB], psb)
                                    sc.op("act", lambda e: e.activation(out=hid[:, g, jc, 0:NCMP], in_=ps[:, 0:NCMP], func=AF.Gelu_apprx_tanh, bias=cv),
                                          [psb, cvecB], [hidB])
                            for g in range(4):
                                ps, psb = pp.get()
                                if kv == 0:
                                    for jc in range(2):
                                        mm(ps[0:64, 0:NCMP], w2t[:, 0, jc, :], hid[:, g, jc, 0:NCMP], jc == 0, jc == 1, [w2B, hidB], psb)
                                    sc.op("dve", lambda e: e.tensor_copy(out=kcT[0:64, g, 0:NCMP], in_=ps[0:64, 0:NCMP]), [psb], [kcTB])
                                else:
                                    for jc in range(2):
                                        mm(ps[0:NCMP, 0:64], hid[:, g, jc, 0:NCMP], w2t[:, 1, jc, :], jc == 0, jc == 1, [w2B, hidB], psb)
                                    sc.op("dve", lambda e: e.tensor_copy(out=vca[0:NCMP, g, 0:64], in_=ps[0:NCMP, 0:64]), [psb], [vcaB])
                        sc.barrier()
                qaug = [mg[0:96, 4 + hh, :] for hh in range(4)]
                qaB = [Buf(f"qaug{hh}") for hh in range(4)]
                qmB = [Buf(f"qmask{hh}") for hh in range(4)]
                kwinT = sbt(ph2, "kwinT", [96, S], BF16); kwinB = Buf("kwinT")
                sc.op("dve", lambda e: e.memset(kwinT[64:96, :], 0.0), [], [kwinB])
                for hh in range(4):
                    sc.op("dve", lambda e: e.memset(qaug[hh][64:96, :], 0.0), [], [qmB[hh]])
                yb = mg[:, 0:4, :].bitcast(F32).rearrange("p a (b d) -> p (a b) d", d=256); ybB = [Buf(f"yb{q}") for q in range(NQ)]
                imp = sbt(ph2, "imp", [128, NT, 32], F32); impB = [Buf(f"imp{q}") for q in range(NQ)]
                pT_rot = Rot(nc, ph2, "pT", [128, 512], BF16, 7)
                e_rot = Rot(nc, ph2, "eT", [128, 512], BF16, 4)
                impf_rot = Rot(nc, ph2, "impf", [128, 32], F32, 3)
                imps_rot = Rot(nc, ph2, "imps", [128, 32], F32, 3)
                xpad = sbt(ph2, "xpad", [128, NT, 96], BF16); xpadB = [Buf(f"xpad{i}") for i in range(NT)]
                ybb_rot = Rot(nc, ph2, "ybb", [128, 256], BF16, 2)
                wk_rot = Rot(nc, ph2, "bwk", [128, KC, 128], BF16, 2)
                etmp_rot = Rot(nc, ph2, "etmp", [128, 4, 64], F32, 3)
                itmp_rot = Rot(nc, ph2, "itmp", [128, 4, 32], F32, 2)
                sc.op("dve", lambda e: e.memset(xpad[:, :, 0:64], 0.0), [], xpadB)

                def epilogue(ps, psb, Q, hh, h, br, firstbr):
                    W = 97 if br == 0 else 65
                    sm, smB = small.get()
                    p3 = ps[:, 0:4 * W].rearrange("p (j w) -> p j w", j=4)
                    sc.op("dve", lambda e: e.tensor_scalar(out=sm[:, 0:4], in0=p3[:, :, 64], scalar1=1e-30, scalar2=None, op0=ALU.add), [psb], [smB])
                    sc.op("dve", lambda e: e.reciprocal(out=sm[:, 0:4], in_=sm[:, 0:4]), [smB], [smB])
                    sc.op("dve", lambda e: e.tensor_tensor(out=sm[:, 4:8], in0=sm[:, 0:4], in1=gates[:, 4 * Q:4 * Q + 4, h * 3 + br], op=ALU.mult),
                          [smB, gatesB], [smB])
                    dst = yb[:, 4 * Q:4 * Q + 4, hh * 64:(hh + 1) * 64]
                    cb = sm[:, 4:8].unsqueeze(2).to_broadcast([128, 4, 64])
                    if firstbr:
                        sc.op("dve", lambda e: e.tensor_tensor(out=dst, in0=p3[:, :, 0:64], in1=cb, op=ALU.mult), [psb, smB], [ybB[Q]])
                    else:
                        tm, tmB = etmp_rot.get()
                        sc.op("dve", lambda e: e.tensor_tensor(out=tm[:], in0=p3[:, :, 0:64], in1=cb, op=ALU.mult), [psb, smB], [tmB])
                        sc.op("pool", lambda e: e.tensor_tensor(out=dst, in0=dst, in1=tm[:], op=ALU.add), [tmB, ybB[Q]], [ybB[Q]])
                    if br == 0:
                        di = imp[:, 4 * Q:4 * Q + 4, :]
                        rb = sm[:, 0:4].unsqueeze(2).to_broadcast([128, 4, 32])
                        if hh == 0:
                            sc.op("dve", lambda e: e.tensor_tensor(out=di, in0=p3[:, :, 65:97], in1=rb, op=ALU.mult), [psb, smB], [impB[Q]])
                        else:
                            tm, tmB = itmp_rot.get()
                            sc.op("dve", lambda e: e.tensor_tensor(out=tm[:], in0=p3[:, :, 65:97], in1=rb, op=ALU.mult), [psb, smB], [tmB])
                            sc.op("pool", lambda e: e.tensor_tensor(out=di, in0=di, in1=tm[:], op=ALU.add), [tmB, impB[Q]], [impB[Q]])

                def attn_items(kind):
                    items = []
                    for hh in range(4):
                        for Q in range(NQ):
                            kts = list(range(4 * Q + 4)) if kind == "slc" else list(range(max(0, 4 * Q - 4), 4 * Q + 4))
                            for n_, kt in enumerate(kts):
                                r = kt - 4 * Q
                                jlo = max(0, r)
                                jhi = 3 if kind == "slc" else min(3, r + 4)
                                items.append(dict(kind=kind, hh=hh, Q=Q, kt=kt, r=r, jlo=jlo, jhi=jhi, first=(n_ == 0), last=(n_ == len(kts) - 1)))
                    return items

                def stage1(it, g):
                    hh, Q, kt, r, jlo, jhi = it["hh"], it["Q"], it["kt"], it["r"], it["jlo"], it["jhi"]
                    N = 128 * (jhi - jlo + 1)
                    c0 = Q * 512 + 128 * jlo
                    ps, psb = pp.get()
                    diag = r >= 0
                    anti = it["kind"] == "win" and r <= -1
                    if it["kind"] == "slc":
                        mm(ps[:, 0:N], kaug[0:96, kt * 128:(kt + 1) * 128], qaug[hh][0:96, c0:c0 + N], True, not (diag or anti), [kaugB, qaB[hh], qmB[hh]], psb)
                    else:
                        mm(ps[:, 0:N], kwinT[0:96, kt * 128:(kt + 1) * 128], qaug[hh][0:96, c0:c0 + N], True, not (diag or anti), [kwinB, qaB[hh], qmB[hh]], psb)
                    if diag:
                        mm(ps[:, 0:128], identb[:], negcaus[:], False, True, [constB], psb)
                    if anti:
                        mm(ps[:, N - 128:N], identb[:], neganti[:], False, True, [constB], psb)
                    pT, pTB = pT_rot.get()
                    sc.op("act", lambda e: e.activation(out=pT[:, 0:N], in_=ps[:, 0:N], func=AF.Exp, scale=0.125), [psb], [pTB])
                    it["pT"] = (pT, pTB)

                def stage2(it, g, state, post=None):
                    hh, Q, kt, jlo, jhi = it["hh"], it["Q"], it["kt"], it["jlo"], it["jhi"]
                    if it["first"]:
                        state["po"] = acc.get()
                    po, pob = state["po"]
                    pT, pTB = it["pT"]
                    vt, vtB = (vslc, vslcB) if it["kind"] == "slc" else (vwin, vwinB)
                    for j in range(jlo, jhi + 1):
                        mm(po[:, j * 65:(j + 1) * 65], pT[:, (j - jlo) * 128:(j - jlo + 1) * 128], vt[:, kt, g, :],
                           it["first"] and j == jlo, it["last"] and j == jhi, [pTB, vtB], pob)
                    if it["last"]:
                        epilogue(po, pob, Q, hh, 4 * g + hh, 1 if it["kind"] == "slc" else 2, False)
                        if post is not None:
                            post(hh * NQ + Q)

                def run_items(items, g, look=4, post=None):
                    state = {}
                    for n_ in range(min(look, len(items))):
                        stage1(items[n_], g)
                    for n_, it in enumerate(items):
                        if n_ + look < len(items):
                            stage1(items[n_ + look], g)
                        stage2(it, g, state, post)

                for g in range(4):
                    wq, wqB = wrot.get()
                    load_w(wq[:], wqB, wview(w_in[l], 0, D, OFF_Q + g * 256, 256), KC, 256)
                    for hp in range(2):
                        for Q in range(NQ):
                            ps, psb = pp.get()
                            proj_fm(ps, psb, wq[:, :, hp * 128:(hp + 1) * 128], wqB, Q, M=128)
                            for hb in range(2):
                                hh = 2 * hp + hb
                                if (hp + Q) % 2 == 0:
                                    sc.op("act", lambda e: e.copy(out=qaug[hh][0:64, Q * 512:(Q + 1) * 512], in_=ps[hb * 64:(hb + 1) * 64, :]), [psb], [qaB[hh]])
                                else:
                                    sc.op("dve", lambda e: e.tensor_copy(out=qaug[hh][0:64, Q * 512:(Q + 1) * 512], in_=ps[hb * 64:(hb + 1) * 64, :]), [psb], [qaB[hh]])
                    wk, wkB = wk_rot.get()
                    load_w(wk[:, :, 0:64], wkB, wview(w_in[l], 0, D, OFF_KV + 2 * 256 + g * 64, 64), KC, 64)
                    load_w(wk[:, :, 64:128], wkB, wview(w_in[l], 0, D, OFF_KV + 4 * 256 + g * 64, 64), KC, 64)
                    for Q in range(NQ):
                        ps, psb = pp.get()
                        proj_fm(ps, psb, wk[:, :, :], wkB, Q, M=128)
                        sc.op("act", lambda e: e.copy(out=kaug[0:64, Q * 512:(Q + 1) * 512], in_=ps[0:64, :]), [psb], [kaugB])
                        sc.op("act", lambda e: e.copy(out=kwinT[0:64, Q * 512:(Q + 1) * 512], in_=ps[64:128, :]), [psb], [kwinB])
                    cctx = {}

                    def cmp_s1(u):
                        hh, Q = divmod(u, NQ)
                        ps, psb = pp.get()
                        mm(ps[:, :], kcT[:, g, :], qaug[hh][0:96, Q * 512:(Q + 1) * 512], True, False, [kcTB, qaB[hh], qmB[hh]], psb)
                        mm(ps[:, :], identb[:], validb[:, Q * 512:(Q + 1) * 512], False, True, [constB], psb)
                        et, etB = e_rot.get()
                        sc.op("act", lambda e: e.activation(out=et[:, :], in_=ps[:, :], func=AF.Exp, scale=0.125), [psb], [etB])
                        cctx[u] = (et, etB)

                    def cmp_s2(u):
                        hh, Q = divmod(u, NQ)
                        et, etB = cctx.pop(u)
                        ps2, ps2b = pp.get()
                        for j in range(4):
                            mm(ps2[:, j * 97:(j + 1) * 97], et[:, j * 128:(j + 1) * 128], vca[:, g, :], True, True, [etB, vcaB], ps2b)
                        epilogue(ps2, ps2b, Q, hh, 4 * g + hh, 0, True)
                    skew(4 * NQ, [cmp_s1, cmp_s2], lag=3)
                    def sel_tile(i):
                        ip, ipB = impf_rot.get()
                        ip2, ip2B = imps_rot.get()
                        sm, smB = small.get()
                        sm2, sm2B = small.get()
                        sc.op("dve", lambda e: e.tensor_tensor(out=ip[:], in0=imp[:, i, :], in1=keepm[:, i * 32:(i + 1) * 32], op=ALU.mult), [impB[i // 4], constB], [ipB])
                        sc.op("dve", lambda e: e.tensor_tensor(out=ip[:], in0=ip[:], in1=addm[:, i * 32:(i + 1) * 32], op=ALU.add), [ipB, constB], [ipB])
                        sc.op("dve", lambda e: e.max(out=sm[:], in_=ip[:]), [ipB], [smB])
                        sc.op("dve", lambda e: e.match_replace(out=ip2[:], in_to_replace=sm[:], in_values=ip[:], imm_value=-3.0e38), [ipB, smB], [ip2B])
                        sc.op("dve", lambda e: e.max(out=sm2[:], in_=ip2[:]), [ip2B], [sm2B])
                        sc.op("dve", lambda e: e.scalar_tensor_tensor(out=ip[:], in0=ip[:], scalar=sm2[:, 7:8], in1=lem[:, i * 32:(i + 1) * 32], op0=ALU.is_ge, op1=ALU.mult),
                              [ipB, sm2B, constB], [ipB])
                        sc.op("dve", lambda e: e.tensor_scalar(out=xpad[:, i, 64:96], in0=ip[:], scalar1=NEGM, scalar2=-NEGM, op0=ALU.mult, op1=ALU.add), [ipB], [xpadB[i]])
                    run_items(attn_items("win"), g, post=sel_tile)
                    for Q in range(NQ):
                        ps, psb = pp.get()
                        for j in range(4):
                            i = 4 * Q + j
                            mm(ps[0:96, j * 128:(j + 1) * 128], xpad[:, i, :], identb[:], True, True, [xpadB[i], constB], psb)
                        for hh in range(4):
                            sc.op("dve", lambda e: e.tensor_copy(out=qaug[hh][64:96, Q * 512:(Q + 1) * 512], in_=ps[64:96, :]), [psb], [qmB[hh]])
                    run_items(attn_items("slc"), g)
                    for Q in range(NQ):
                        ps, psb = pp.get()
                        pvb = ps[:].bitcast(BF16)
                        for j in range(4):
                            i = 4 * Q + j
                            ybb, ybbB = ybb_rot.get()
                            sc.op("act", lambda e: e.copy(out=ybb[:], in_=yb[:, i, :]), [ybB[Q]], [ybbB])
                            for cc in range(2):
                                sc.op("pe", lambda e: e.transpose(out=pvb[:, cc * 512 + j * 128:cc * 512 + (j + 1) * 128], in_=ybb[:, cc * 128:(cc + 1) * 128], identity=identb[:]),
                                      [ybbB, constB], [psb], inc=(cc == 1))
                        sc.op("dve", lambda e: e.tensor_copy(out=ybT[:, 2 * g:2 * g + 2, Q * 512:(Q + 1) * 512], in_=pvb.rearrange("p (c n) -> p c n", c=2)), [psb], [ybTB[Q]])
                sc.barrier()
                ph2.close()
                tap("ybT", ybT[:], ybTB[0])
                merge_branch(l, 1, ybT, lambda Q: [ybTB[Q]], first)

        def resid_load(i, xsrc, src_is_scratch, rots):
            xt, xtB = rots[0].get()
            rd = [xsB[i]] if src_is_scratch else []
            sc.dma("sp", xt[:], xsrc[i * 128:(i + 1) * 128, :], rd, [xtB], xtB)
            return xt, xtB

        def resid_tile(i, halves, xload, xdst, dst_is_scratch, g_post, g_postB, g_next, g_nextB, rots, pp, tapname=None):
            xt_rot, tmp_rot, xn_rot, hb_rot, jk_rot = rots
            xt, xtB = xload
            sm, smB = small.get()
            jk, jkB = jk_rot.get()
            for hf, (ps, psb) in enumerate(halves):
                sc.op("act", lambda e: e.activation(out=jk[:, hf * 512:(hf + 1) * 512], in_=ps[:], func=AF.Square, accum_out=sm[:, hf:hf + 1]), [psb], [jkB, smB])
            sc.op("dve", lambda e: e.tensor_tensor(out=sm[:, 2:3], in0=sm[:, 0:1], in1=sm[:, 1:2], op=ALU.add), [smB], [smB])
            rstd_from_ss(sm[:, 2:3], smB, D)
            tm, tmB = tmp_rot.get()
            for hf, (ps, psb) in enumerate(halves):
                sc.op("dve", lambda e: e.scalar_tensor_tensor(out=tm[:, hf * 512:(hf + 1) * 512], in0=ps[:], scalar=sm[:, 2:3], in1=g_post[:, hf * 512:(hf + 1) * 512],
                                                              op0=ALU.mult, op1=ALU.mult), [psb, smB, g_postB], [tmB])
            xn, xnB = xn_rot.get()
            sc.op("pool", lambda e: e.tensor_tensor(out=xn[:], in0=tm[:], in1=xt[:], op=ALU.add), [tmB, xtB], [xnB])
            wr = [xsB[i]] if dst_is_scratch else []
            sc.dma("pool", xdst[i * 128:(i + 1) * 128, :], xn[:], [xnB], wr, xnB)
            if debug and tapname is not None:
                sc.dma("pool", dbg[tapname][i * 128:(i + 1) * 128, :], xn[:], [xnB], [], xnB)
            return xn, xnB

        def resid_tile2(i, xn, xnB, g_next, g_nextB, rots, pp):
            xt_rot, tmp_rot, xn_rot, hb_rot, jk_rot = rots
            if g_next is not None:
                return norm_a(xn[:], xnB, g_next, g_nextB, hb_rot, jk_rot)
            return None

        def resid_tile3(i, hbp, pp):
            if hbp is not None:
                norm_b(hbp[0], hbp[1], i, pp)

        def resid_rots(ph):
            return (Rot(nc, ph, "rxt", [128, D], F32, 3), Rot(nc, ph, "rtm", [128, D], F32, 2), Rot(nc, ph, "rxn", [128, D], F32, 3),
                    Rot(nc, ph, "rhb", [128, D], BF16, 3), Rot(nc, ph, "rjk", [128, D], BF16, 2))

        def phase_r1(l):
            with ExitStack() as ph:
                wo = sbt(ph, "wo", [128, KC, D], BF16); woB = [Buf(f"wo{j}") for j in range(4)]
                rots = resid_rots(ph)
                pp = PsPool(range(8))
                for j in range(4):
                    load_w(wo[:, :, j * 256:(j + 1) * 256], woB[j], wview(wo_d[l], 0, D, j * 256, 256), KC, 256)
                g_post, g_postB = load_gbc(l, 1)
                g_next, g_nextB = load_gbc(l, 2)
                xsrc = x_in if l == 0 else xs_d
                def s1(i):
                    halves = []
                    for hf in range(2):
                        ps, psb = pp.get()
                        for c in range(KC):
                            mm(ps[:], mg[:, c, i * 128:(i + 1) * 128], wo[:, c, hf * 512:(hf + 1) * 512], c == 0, c == KC - 1,
                               [mgB[i // 4], woB[2 * hf], woB[2 * hf + 1]], psb)
                        halves.append((ps, psb))
                    return halves
                rc = {}

                def r_a(i):
                    rc[i] = (s1(i), resid_load(i, xsrc, l > 0, rots))

                def r_b(i):
                    rc[i] = resid_tile(i, rc[i][0], rc[i][1], xs_d, True, g_post, g_postB, g_next, g_nextB, rots, pp, tapname=("x1" if l == 0 else None))

                def r_c(i):
                    xn, xnB = rc[i]
                    rc[i] = resid_tile2(i, xn, xnB, g_next, g_nextB, rots, pp)

                def r_d(i):
                    resid_tile3(i, rc.pop(i), pp)
                skew(NT, [r_a, r_b, r_c, r_d])
                sc.barrier()

        def phase_f(l, last):
            with ExitStack() as ph:
                actT = sbt(ph, "actT", [128, NJ, 1024], BF16); actB = [Buf(f"actT{q}") for q in range(2)]
                wout_lo = mg[:].rearrange("p c (a n) -> p (c a) n", a=2)
                wout_hi = sbt(ph, "wout_hi", [128, NJ - 16, D], BF16)
                woutB = [Buf(f"wout{jp}") for jp in range(NJ // 2)]
                g_post, g_postB = load_gbc(l, 3)
                g_next, g_nextB = (None, None)
                for th in range(2):
                    with ExitStack() as ph2:
                        wrot = Rot(nc, ph2, "fw", [128, KC, 256], BF16, 4)
                        sg_rot = Rot(nc, ph2, "fsg", [128, 512], F32, 2)
                        pp = PsPool(range(8))
                        def fload(jp):
                            wg, wgB = wrot.get()
                            load_w(wg[:], wgB, wview(wfi_d[l], 0, D, jp * 256, 256), KC, 256, cast_eng="dve")
                            wu, wuB = wrot.get()
                            load_w(wu[:], wuB, wview(wfi_d[l], 0, D, DFF + jp * 256, 256), KC, 256, cast_eng="dve")
                            return wg, wgB, wu, wuB
                        nxt = fload(0)
                        for jp in range(NJ // 2):
                            wg, wgB, wu, wuB = nxt
                            if jp + 1 < NJ // 2:
                                nxt = fload(jp + 1)
                            for cc in range(2):
                                j = 2 * jp + cc
                                for qq in range(2):
                                    Q = 2 * th + qq
                                    ps, psb = pp.get()
                                    proj_fm(ps, psb, wg[:, :, cc * 128:(cc + 1) * 128], wgB, Q)
                                    sg, sgB = sg_rot.get()
                                    sc.op("act", lambda e: e.activation(out=sg[:], in_=ps[:], func=AF.Silu), [psb], [sgB])
                                    ps2, ps2b = pp.get()
                                    proj_fm(ps2, ps2b, wu[:, :, cc * 128:(cc + 1) * 128], wuB, Q)
                                    sc.op("dve", lambda e: e.tensor_tensor(out=actT[:, j, qq * 512:(qq + 1) * 512], in0=ps2[:], in1=sg[:], op=ALU.mult),
                                          [ps2b, sgB], [actB[qq]])
                        sc.barrier()
                    with ExitStack() as ph2:
                        rots = resid_rots(ph2)
                        pp = PsPool(range(8))
                        if th == 0:
                            for jp in range(NJ // 2):
                                dst = wout_lo[:, 2 * jp:2 * jp + 2, :] if jp < 8 else wout_hi[:, 2 * jp - 16:2 * jp - 14, :]
                                load_w(dst, woutB[jp], wview(wfo_d[l], jp * 256, 256, 0, D), 2, D, cast_eng="dve")
                        if not last and g_next is None:
                            g_next, g_nextB = load_gbc(l + 1, 0)
                        def s1(ii):
                            halves = []
                            for hf in range(2):
                                ps, psb = pp.get()
                                for j in range(NJ):
                                    wj = wout_lo[:, j, hf * 512:(hf + 1) * 512] if j < 16 else wout_hi[:, j - 16, hf * 512:(hf + 1) * 512]
                                    mm(ps[:], actT[:, j, ii * 128:(ii + 1) * 128], wj, j == 0, j == NJ - 1, [actB[ii // 4], woutB[j // 2]], psb)
                                halves.append((ps, psb))
                            return halves
                        rc = {}

                        def r_a(ii):
                            rc[ii] = (s1(ii), resid_load(8 * th + ii, xs_d, True, rots))

                        def r_b(ii):
                            rc[ii] = resid_tile(8 * th + ii, rc[ii][0], rc[ii][1], out_d if last else xs_d, not last, g_post, g_postB, g_next, g_nextB, rots, pp,
                                                tapname=("x2" if l == 0 else None))

                        def r_c(ii):
                            xn, xnB = rc[ii]
                            rc[ii] = resid_tile2(8 * th + ii, xn, xnB, g_next, g_nextB, rots, pp)

                        def r_d(ii):
                            resid_tile3(8 * th + ii, rc.pop(ii), pp)
                        skew(8, [r_a, r_b, r_c, r_d])
                        sc.barrier()

        phase_n0()
        for l in range(nlayers):
            if "b" in only:
                phase_b(l, True)
            if "a" in only:
                phase_a(l, "b" not in only)
            if "c" in only:
                phase_c(l, "b" not in only and "a" not in only)
            if "r" in only:
                phase_r1(l)
            if "f" in only:
                phase_f(l, l == nlayers - 1)
        sc.finish()
    return nc


def prep_shared(inp):
    f = lambda a: np.ascontiguousarray(np.asarray(a, dtype=np.float32))
    sh = {}
    sh["w_in"] = f(inp["w_in"])
    sh["gvec"] = f(np.stack([inp["g_pre_mix"], inp["g_post_mix"], inp["g_pre_ffn"], inp["g_post_ffn"],
                             inp["gmlp_ln_g"], inp["gmlp_ln_b"]], axis=1))
    pv = np.zeros((DEPTH, 128, KC, 8), np.float32)
    for l in range(DEPTH):
        def col(v):
            return np.asarray(v, np.float32).reshape(KC, 128).T
        for j in range(4):
            pv[l, :, :, j] = col(inp["rnn_conv_w"][l, j])
        pv[l, :, :, 4] = col(inp["rnn_conv_b"][l])
        pv[l, :, :, 5] = col(inp["rnn_ba"][l])
        pv[l, :, :, 6] = col(inp["rnn_bx"][l])
        pv[l, :, :, 7] = col(inp["rnn_lam"][l])
    sh["pvec"] = f(pv.reshape(DEPTH, 128, KC * 8))
    sh["gmlp_ws"] = f(inp["gmlp_ws"])
    sh["gmlp_bs"] = f(np.asarray(inp["gmlp_bs"]).reshape(DEPTH, 1, 512))
    sh["peT"] = f(np.stack([np.transpose(inp["nsa_pe_k"], (0, 2, 1)), np.transpose(inp["nsa_pe_v"], (0, 2, 1))], axis=1))
    sh["nsa_w1"] = f(np.stack([inp["nsa_wk1"], inp["nsa_wv1"]], axis=1))
    sh["nsa_w2"] = f(np.stack([inp["nsa_wk2"], inp["nsa_wv2"]], axis=1))
    bd = np.zeros((DEPTH, 2, KC, 128, 128), np.float32)
    for a, nm in enumerate(("rnn_wa", "rnn_wx")):
        w = np.asarray(inp[nm], np.float32)
        for c in range(KC):
            bd[:, a, c, 0:64, 0:64] = w[:, 2 * c]
            bd[:, a, c, 64:128, 64:128] = w[:, 2 * c + 1]
    sh["rnn_bd"] = bd
    sh["w_br"] = f(np.stack([inp["w_br_a"], inp["w_br_b"], inp["w_br_c"]], axis=1))
    sh["w_o"] = f(inp["w_o"])
    sh["w_ffn_in"] = f(inp["w_ffn_in"])
    sh["w_ffn_out"] = f(inp["w_ffn_out"])
    p = np.arange(128)
    sq = np.zeros((4, 128, 128), np.float32)
    sq[0] = np.eye(128)
    sq[1] = (p[:, None] >= p[None, :])
    sq[2] = np.where(p[:, None] <= p[None, :], 0.0, -NEGM)
    sq[3] = np.where(p[:, None] > p[None, :], 0.0, -NEGM)
    sh["c_sq"] = sq
    t = np.arange(S)
    n = np.arange(128)
    valid = ((n[:, None] * 16 + 31) <= t[None, :]) & (n[:, None] < NCMP)
    sh["c_valid"] = np.where(valid, 0.0, -NEGM).astype(np.float32)
    sh["c_E"] = (np.arange(32)[:, None] == (t[None, :] // 64)).astype(np.float32)
    cur = (t // 64).reshape(NT, 128).T[:, :, None]
    j = np.arange(32)[None, None, :]
    forced = (j == 0) | (j == cur) | (j == cur - 1)
    future = j > cur
    keep = (~forced & ~future).astype(np.float32)
    add = np.where(forced, 1e4, np.where(future, -1e30, 0.0)).astype(np.float32)
    le = (~future).astype(np.float32)
    sh["c_msk"] = np.stack([keep, add, le]).reshape(3, 128, NT * 32).astype(np.float32)
    cs = np.arange(NCMP) * 16
    ss = np.arange(32) * 64
    ovl = np.clip(np.minimum(cs[:, None] + 32, ss[None, :] + 64) - np.maximum(cs[:, None], ss[None, :]), 0, None) / 32.0
    ov = np.zeros((128, 33), np.float32)
    ov[:NCMP, 0] = 1.0
    ov[:NCMP, 1:] = ovl
    sh["c_ov"] = ov
    return sh


def kernel(**inputs):
    sh = prep_shared(inputs)
    nc = build_program(debug=False)
    x = np.asarray(inputs["x"], np.float32)
    n = x.shape[0]
    in_maps = [dict(sh, x=np.ascontiguousarray(x[b])) for b in range(n)]
    res = run_bass_kernel_spmd(nc, in_maps, core_ids=list(range(n)))
    return np.stack([np.asarray(r["out"], dtype=np.float32) for r in res.results], axis=0)
```
